# Optimizing a Trainium2 kernel written in Bass

```python
import math
import jax, jax.numpy as jnp
from jax import lax
import numpy as np

D_MODEL = 2048
BATCH = 4
SEQ = 2048
DEPTH = 2
DEC_BATCH = 4
DEC_SEQ = 8192
PAST_LEN = 128

N_META = 16
GRID_W = 64
H_A = 4
HD_A = 128
W_A = H_A * 2 * HD_A
H_B = 8
HD_B = 128
W_B = H_B * HD_B
NA_WIN_R = 8
NA_WIN_C = 16
D_FF = 5632
CONV_W = 3
ROPE_THETA = 10000.0
Q_BLOCK = 128
EPS = 1e-6
SPLIT_SIZES = (W_A, W_A, W_A, W_B, W_B, W_B, D_MODEL, D_MODEL)
D_IN = 3 * W_A + 3 * W_B + 2 * D_MODEL

kernel_name = 'hybrid_diff_natten_encoder'


def rms_norm(x, g):
    x32 = x.astype(jnp.float32)
    y = x32 * lax.rsqrt(jnp.mean(x32 * x32, axis=-1, keepdims=True) + EPS)
    return (y * g.astype(jnp.float32)).astype(x.dtype)


def rope_tables(n_pos, dim):
    inv = 1.0 / (ROPE_THETA ** (jnp.arange(0, dim, 2, dtype=jnp.float32) / dim))
    ang = jnp.arange(n_pos, dtype=jnp.float32)[:, None] * inv[None, :]
    return jnp.cos(ang)[:, None, :], jnp.sin(ang)[:, None, :]


def apply_rope(x, cos, sin):
    half = x.shape[-1] // 2
    cos = cos.astype(x.dtype)
    sin = sin.astype(x.dtype)
    x1, x2 = x[..., :half], x[..., half:]
    return jnp.concatenate([x1 * cos - x2 * sin, x2 * cos + x1 * sin], axis=-1)


def lambda_init(layer):
    return 0.8 - 0.6 * math.exp(-0.3 * layer)


def diff_attention(q, k, v, lam):
    B, L = q.shape[0], q.shape[1]
    n_blocks = (L - N_META) // Q_BLOCK
    q = q * (HD_A ** -0.5)

    def attend(qb):
        s = jnp.einsum('bqhcd,bkhcd->bhcqk', qb, k).astype(jnp.float32)
        p = jax.nn.softmax(s, axis=-1)
        p = p[:, :, 0] - lam * p[:, :, 1]
        return jnp.einsum('bhqk,bkhe->bqhe', p.astype(v.dtype), v)

    o_meta = attend(q[:, :N_META])
    q_blocks = q[:, N_META:].reshape(B, n_blocks, Q_BLOCK, H_A, 2, HD_A).swapaxes(0, 1)
    o_real = lax.map(attend, q_blocks)
    o_real = o_real.swapaxes(0, 1).reshape(B, n_blocks * Q_BLOCK, H_A, 2 * HD_A)
    return jnp.concatenate([o_meta, o_real], axis=1)


def neighbourhood_attention(q, k, v, rpb):
    B, L = q.shape[0], q.shape[1]
    T = L - N_META
    rows = T // GRID_W
    wr = min(NA_WIN_R, rows)
    wc = NA_WIN_C
    q = q * (HD_B ** -0.5)
    qm, km, vm = q[:, :N_META], k[:, :N_META], v[:, :N_META]
    grid = (B, rows, GRID_W, H_B, HD_B)
    qg = q[:, N_META:].reshape(grid)
    kg = k[:, N_META:].reshape(grid)
    vg = v[:, N_META:].reshape(grid)
    row_start = np.clip(np.arange(rows) - wr // 2, 0, rows - wr).astype(np.int32)
    col_start = np.clip(np.arange(GRID_W) - wc // 2, 0, GRID_W - wc).astype(np.int32)
    col_idx = (col_start[:, None] + np.arange(wc)[None, :]).astype(np.int32)
    dr = (row_start[:, None] + np.arange(wr)[None, :] - np.arange(rows)[:, None] + (NA_WIN_R - 1)).astype(np.int32)
    dc = (col_idx - np.arange(GRID_W)[:, None] + (NA_WIN_C - 1)).astype(np.int32)
    bias = rpb[:, dr[:, None, :, None], dc[None, :, None, :]].astype(jnp.float32)
    bias = bias.transpose(1, 0, 2, 3, 4)

    def attend_row(args):
        q_r, r0, b_r = args
        k_rows = lax.dynamic_slice_in_dim(kg, r0, wr, axis=1)
        v_rows = lax.dynamic_slice_in_dim(vg, r0, wr, axis=1)
        k_win = k_rows[:, :, col_idx]
        v_win = v_rows[:, :, col_idx]
        s_win = jnp.einsum('bchd,bicjhd->bhcij', q_r, k_win).astype(jnp.float32) + b_r
        s_meta = jnp.einsum('bchd,bmhd->bhcm', q_r, km).astype(jnp.float32)
        s = jnp.concatenate([s_meta, s_win.reshape(B, H_B, GRID_W, wr * wc)], axis=-1)
        p = jax.nn.softmax(s, axis=-1).astype(v.dtype)
        p_meta = p[..., :N_META]
        p_win = p[..., N_META:].reshape(B, H_B, GRID_W, wr, wc)
        return (jnp.einsum('bhcm,bmhd->bchd', p_meta, vm)
                + jnp.einsum('bhcij,bicjhd->bchd', p_win, v_win))

    o_real = lax.map(attend_row, (qg.swapaxes(0, 1), jnp.asarray(row_start), bias))
    o_real = o_real.swapaxes(0, 1).reshape(B, T, H_B, HD_B)
    s_m = jnp.einsum('bqhd,bkhd->bhqk', qm, km).astype(jnp.float32)
    p_m = jax.nn.softmax(s_m, axis=-1).astype(v.dtype)
    o_meta = jnp.einsum('bhqk,bkhd->bqhd', p_m, vm)
    return jnp.concatenate([o_meta, o_real], axis=1)


def conv_ffn(h, w_up, cw, cb, w_down):
    gate, val = jnp.split(h @ w_up, 2, axis=-1)
    L = gate.shape[1]
    pad = CONV_W // 2
    gp = jnp.pad(gate, ((0, 0), (pad, pad), (0, 0)))
    acc = cb
    for j in range(CONV_W):
        acc = acc + gp[:, j:j + L] * cw[j]
    return (jax.nn.gelu(acc, approximate=True) * val) @ w_down


def encoder_layer(x, p, l, cos, sin):
    B, L, _ = x.shape
    f32 = jnp.float32
    h = rms_norm(x, p['norm_mix_pre'][l])
    splits = np.cumsum(SPLIT_SIZES)[:-1].tolist()
    q_a, k_a, v_a, q_b, k_b, v_b, g_a, g_b = jnp.split(h @ p['w_in'][l], splits, axis=-1)
    q_a = apply_rope(q_a.reshape(B, L, 2 * H_A, HD_A), cos, sin).reshape(B, L, H_A, 2, HD_A)
    k_a = apply_rope(k_a.reshape(B, L, 2 * H_A, HD_A), cos, sin).reshape(B, L, H_A, 2, HD_A)
    v_a = v_a.reshape(B, L, H_A, 2 * HD_A)
    lam0 = lambda_init(l)
    lam = (jnp.exp(jnp.sum(p['lam_q1'][l].astype(f32) * p['lam_k1'][l].astype(f32)))
           - jnp.exp(jnp.sum(p['lam_q2'][l].astype(f32) * p['lam_k2'][l].astype(f32))) + lam0)
    o_a = diff_attention(q_a, k_a, v_a, lam)
    o_a = rms_norm(o_a, p['subln'][l]) * (1.0 - lam0)
    o_b = neighbourhood_attention(q_b.reshape(B, L, H_B, HD_B), k_b.reshape(B, L, H_B, HD_B),
                                  v_b.reshape(B, L, H_B, HD_B), p['rpb'][l])
    mixed = (jax.nn.sigmoid(g_a) * (o_a.reshape(B, L, W_A) @ p['w_br_a'][l])
             + jax.nn.sigmoid(g_b) * (o_b.reshape(B, L, W_B) @ p['w_br_b'][l]))
    x = x + rms_norm(mixed @ p['w_out'][l], p['norm_mix_post'][l])
    h = rms_norm(x, p['norm_ffn_pre'][l])
    f = conv_ffn(h, p['w_ffn_up'][l], p['conv_w'][l], p['conv_b'][l], p['w_ffn_down'][l])
    return x + rms_norm(f, p['norm_ffn_post'][l])


def encode(x, p):
    B, T, _ = x.shape
    L = T + N_META
    meta = jnp.broadcast_to(p['meta_tokens'][None].astype(x.dtype), (B, N_META, D_MODEL))
    h = jnp.concatenate([meta, x], axis=1)
    cos, sin = rope_tables(L, HD_A)
    for l in range(DEPTH):
        h = encoder_layer(h, p, l, cos, sin)
    return h[:, N_META:]


def setup_inputs(seed: int = 0) -> dict:
    key = jax.random.key(seed)
    ks = jax.random.split(key, 24)

    def nrm(k, shape, scale):
        return jax.random.normal(k, shape, jnp.float32) * scale

    def gain(k, shape):
        return 1.0 + 0.05 * jax.random.normal(k, shape, jnp.float32)

    return {
        'x_prompt': nrm(ks[0], (BATCH, SEQ, D_MODEL), 1.0),
        'x_sample': nrm(ks[1], (DEC_BATCH, DEC_SEQ, D_MODEL), 1.0),
        'meta_tokens': nrm(ks[2], (N_META, D_MODEL), 1.0),
        'norm_mix_pre': gain(ks[3], (DEPTH, D_MODEL)),
        'w_in': nrm(ks[4], (DEPTH, D_MODEL, D_IN), D_MODEL ** -0.5),
        'lam_q1': nrm(ks[5], (DEPTH, HD_A), 0.1),
        'lam_k1': nrm(ks[6], (DEPTH, HD_A), 0.1),
        'lam_q2': nrm(ks[7], (DEPTH, HD_A), 0.1),
        'lam_k2': nrm(ks[8], (DEPTH, HD_A), 0.1),
        'subln': gain(ks[9], (DEPTH, 2 * HD_A)),
        'rpb': nrm(ks[10], (DEPTH, H_B, 2 * NA_WIN_R - 1, 2 * NA_WIN_C - 1), 0.05),
        'w_br_a': nrm(ks[11], (DEPTH, W_A, D_MODEL), W_A ** -0.5),
        'w_br_b': nrm(ks[12], (DEPTH, W_B, D_MODEL), W_B ** -0.5),
        'w_out': nrm(ks[13], (DEPTH, D_MODEL, D_MODEL), D_MODEL ** -0.5),
        'norm_mix_post': gain(ks[14], (DEPTH, D_MODEL)),
        'norm_ffn_pre': gain(ks[15], (DEPTH, D_MODEL)),
        'w_ffn_up': nrm(ks[16], (DEPTH, D_MODEL, 2 * D_FF), D_MODEL ** -0.5),
        'conv_w': nrm(ks[17], (DEPTH, CONV_W, D_FF), CONV_W ** -0.5),
        'conv_b': nrm(ks[18], (DEPTH, D_FF), 0.01),
        'w_ffn_down': nrm(ks[19], (DEPTH, D_FF, D_MODEL), D_FF ** -0.5),
        'norm_ffn_post': gain(ks[20], (DEPTH, D_MODEL)),
    }


def reference(x_prompt, x_sample, meta_tokens, norm_mix_pre, w_in, lam_q1, lam_k1, lam_q2, lam_k2,
              subln, rpb, w_br_a, w_br_b, w_out, norm_mix_post, norm_ffn_pre, w_ffn_up, conv_w,
              conv_b, w_ffn_down, norm_ffn_post):
    p = dict(meta_tokens=meta_tokens, norm_mix_pre=norm_mix_pre, w_in=w_in, lam_q1=lam_q1,
             lam_k1=lam_k1, lam_q2=lam_q2, lam_k2=lam_k2, subln=subln, rpb=rpb, w_br_a=w_br_a,
             w_br_b=w_br_b, w_out=w_out, norm_mix_post=norm_mix_post, norm_ffn_pre=norm_ffn_pre,
             w_ffn_up=w_ffn_up, conv_w=conv_w, conv_b=conv_b, w_ffn_down=w_ffn_down,
             norm_ffn_post=norm_ffn_post)
    y_prompt = encode(x_prompt, p)
    y_sample = encode(x_sample, p)
    return (y_prompt, y_sample)
```

```python
import math
import numpy as np
from contextlib import ExitStack
import concourse.bass as bass
import concourse.mybir as mybir
from concourse.bass_utils import run_bass_kernel_spmd

F32 = mybir.dt.float32
BF16 = mybir.dt.bfloat16
AF = mybir.ActivationFunctionType
ALU = mybir.AluOpType
AX = mybir.AxisListType

D = 2048
DIN = 10240
DFF = 5632
NM = 16
EPS = 1e-6
SCALE = 128.0 ** -0.5
NEG = -30000.0
ENGS = ("pe", "act", "dve", "pool", "sp")


class Res:
    __slots__ = ("name", "w", "r", "fill", "drain", "excl")

    def __init__(self, name, excl=False):
        self.name = name
        self.excl = excl
        self.w = None
        self.r = []
        self.fill = None
        self.drain = None


class SemSlot:
    __slots__ = ("sem", "cnt", "idx")

    def __init__(self, sem, idx):
        self.sem = sem
        self.cnt = 0
        self.idx = idx


class K:
    def __init__(self, nc, es, n_dma_sems=90):
        self.nc = nc
        self.recs = {e: [] for e in ENGS}
        self.seen = {e: {} for e in ENGS}
        self.esem = {}
        for e in ("pe", "act", "dve", "pool"):
            self.esem[e] = es.enter_context(nc.semaphore("es_" + e))
        self.slots = [SemSlot(es.enter_context(nc.semaphore("ds%d" % i)), i) for i in range(n_dma_sems)]
        self.free_slots = {"hw": list(self.slots[:n_dma_sems // 2]), "sw": list(self.slots[n_dma_sems // 2:])}
        self.kind = {}
        self.nins = 0

    def get_slot(self, q):
        kind = "sw" if q == "pool" else "hw"
        s = self.free_slots[kind].pop()
        self.kind[s.idx] = kind
        return s

    def release(self, res_list):
        for r in res_list:
            for s in (r.fill, r.drain):
                if s is not None:
                    self.free_slots[self.kind[s.idx]].append(s)
            r.fill = None
            r.drain = None

    def _collect(self, eng, reads, writes):
        deps = []
        for r in reads:
            if r.w is not None:
                deps.append(r.w)
        for w in writes:
            if w.w is not None:
                deps.append(w.w)
            deps.extend(w.r)
        waits = []
        seen = self.seen[eng]
        for d in deps:
            if d[0] == "E":
                x, val = d[1], d[2]
                if x == eng and eng in ("pe", "sp"):
                    continue
                key = ("E", x)
            else:
                val = d[2]
                key = ("S", d[1].idx)
            if seen.get(key, -1) >= val:
                continue
            seen[key] = val
            waits.append(d)
        return waits

    def ins(self, eng, fn, reads=(), writes=()):
        if any(r.excl for r in reads):
            writes = tuple(writes) + tuple(r for r in reads if r.excl)
            reads = tuple(r for r in reads if not r.excl)
        waits = self._collect(eng, reads, writes)
        idx = len(self.recs[eng])
        self.recs[eng].append({"fn": fn, "waits": waits, "signal": False, "dma": None})
        ev = ("E", eng, idx)
        for r in reads:
            r.r.append(ev)
        for w in writes:
            w.w = ev
            w.r = []
        self.nins += 1
        return ev

    def dma(self, q, fn, reads=(), writes=()):
        waits = self._collect(q, reads, writes)
        if writes:
            tgt = writes[0]
            if tgt.fill is None:
                tgt.fill = self.get_slot(q)
            slot = tgt.fill
        else:
            tgt = reads[0]
            if tgt.drain is None:
                tgt.drain = self.get_slot(q)
            slot = tgt.drain
        slot.cnt += 16
        ev = ("S", slot, slot.cnt)
        self.recs[q].append({"fn": fn, "waits": waits, "signal": False, "dma": slot})
        for r in reads:
            r.r.append(ev)
        for w in writes:
            w.w = ev
            w.r = []
        self.nins += 1
        return ev

    def barrier(self):
        last = {}
        for e in ("pe", "act", "dve", "pool"):
            for j in range(len(self.recs[e]) - 1, -1, -1):
                rec = self.recs[e][j]
                if rec["dma"] is None and rec["fn"] is not None:
                    last[e] = j
                    break
        for e in ENGS:
            waits = []
            seen = self.seen[e]
            for x, j in last.items():
                if x == e:
                    continue
                key = ("E", x)
                if seen.get(key, -1) >= j:
                    continue
                seen[key] = j
                waits.append(("E", x, j))
            for s in self.slots:
                if s.cnt > 0:
                    key = ("S", s.idx)
                    if seen.get(key, -1) >= s.cnt:
                        continue
                    seen[key] = s.cnt
                    waits.append(("S", s, s.cnt))
            if waits:
                self.recs[e].append({"fn": None, "waits": waits, "signal": False, "dma": None})

    def emit(self):
        nc = self.nc
        recs = self.recs
        for e in ENGS:
            for rec in recs[e]:
                for d in rec["waits"]:
                    if d[0] == "E":
                        recs[d[1]][d[2]]["signal"] = True
        val = {}
        for e in ("pe", "act", "dve", "pool"):
            c = 0
            for j, rec in enumerate(recs[e]):
                if rec["signal"]:
                    c += 1
                    val[(e, j)] = c
        esem = self.esem

        def run(e, eng):
            for rec in recs[e]:
                for d in rec["waits"]:
                    if d[0] == "E":
                        eng.wait_ge(esem[d[1]], val[(d[1], d[2])])
                    else:
                        eng.wait_ge(d[1].sem, d[2])
                if rec["fn"] is None:
                    continue
                ins = rec["fn"](eng)
                if rec["dma"] is not None:
                    ins.then_inc(rec["dma"].sem, 16)
                elif rec["signal"]:
                    ins.then_inc(esem[e], 1)

        with nc.Block() as block:
            @block.tensor
            def _(eng):
                run("pe", eng)

            @block.scalar
            def _(eng):
                run("act", eng)

            @block.vector
            def _(eng):
                run("dve", eng)

            @block.gpsimd
            def _(eng):
                run("pool", eng)

            @block.sync
            def _(eng):
                run("sp", eng)


WSPEC = [
    ("meta_tokens", (NM, D)), ("norm_mix_pre", (2, D)), ("w_in", (2, D, DIN)),
    ("lam_q1", (2, 128)), ("lam_k1", (2, 128)), ("lam_q2", (2, 128)), ("lam_k2", (2, 128)),
    ("subln", (2, 256)), ("rpb", (2, 8, 15, 31)), ("w_br_a", (2, 1024, D)), ("w_br_b", (2, 1024, D)),
    ("w_out", (2, D, D)), ("norm_mix_post", (2, D)), ("norm_ffn_pre", (2, D)),
    ("w_ffn_up", (2, D, 2 * DFF)), ("conv_w", (2, 3, DFF)), ("conv_b", (2, DFF)),
    ("w_ffn_down", (2, DFF, D)), ("norm_ffn_post", (2, D)),
]


def lam_init(l):
    return 0.8 - 0.6 * math.exp(-0.3 * l)


def build_program(jobs, depth=2, phases=None, dbg=()):
    nc = bass.Bass("TRN2", target_bir_lowering=False)
    Wd = {n: nc.dram_tensor(n, list(s), F32, kind="ExternalInput") for n, s in WSPEC}
    W = {n: t.ap() for n, t in Wd.items()}
    xin = [nc.dram_tensor("x%d" % j, [T, D], F32, kind="ExternalInput").ap() for j, T in enumerate(jobs)]
    yout = [nc.dram_tensor("y%d" % j, [T, D], F32, kind="ExternalOutput").ap() for j, T in enumerate(jobs)]
    cs_in = [nc.dram_tensor("cs%d" % j, [2, 128, T + NM], F32, kind="ExternalInput").ap() for j, T in enumerate(jobs)]
    c_rot = nc.dram_tensor("c_rot", [128, 128], F32, kind="ExternalInput").ap()
    c_j64 = nc.dram_tensor("c_j64", [64, 64], F32, kind="ExternalInput").ap()
    c_mask = nc.dram_tensor("c_mask", [64, 512], F32, kind="ExternalInput").ap()

    wbf = {}
    for l in range(depth):
        wbf["in", l] = nc.dram_tensor("wb_in%d" % l, [20, 128, 16, 512], BF16).ap()
        wbf["bra", l] = nc.dram_tensor("wb_bra%d" % l, [4, 128, 8, 512], BF16).ap()
        wbf["brb", l] = nc.dram_tensor("wb_brb%d" % l, [4, 128, 8, 512], BF16).ap()
        wbf["out", l] = nc.dram_tensor("wb_out%d" % l, [4, 128, 16, 512], BF16).ap()
        wbf["up", l] = nc.dram_tensor("wb_up%d" % l, [22, 128, 16, 512], BF16).ap()
        wbf["dn", l] = nc.dram_tensor("wb_dn%d" % l, [16, 128, 11, 512], BF16).ap()
    rpbpad_t = nc.dram_tensor("rpbpad", [depth * 8 * 15 * 31 + 256], F32)

    SC = []
    for j, T in enumerate(jobs):
        L = T + NM
        s = {}
        def dk(n):
            return "ExternalOutput" if n in dbg else "Internal"
        s["xA"] = nc.dram_tensor("xA%d" % j, [16, 128, L], F32, kind=dk("xA")).ap()
        s["xB"] = nc.dram_tensor("xB%d" % j, [16, 128, L], F32, kind=dk("xB")).ap()
        for n in ("qaT", "kaT", "qbT", "kbT", "oaT", "obT"):
            s[n] = nc.dram_tensor("%s%d" % (n, j), [8, 128, L], BF16, kind=dk(n)).ap()
        for n in ("gaT", "gbT"):
            s[n] = nc.dram_tensor("%s%d" % (n, j), [16, 128, L], BF16, kind=dk(n)).ap()
        for n in ("va", "vb"):
            s[n] = nc.dram_tensor("%s%d" % (n, j), [L, 1024], BF16, kind=dk(n)).ap()
        SC.append(s)

    es = ExitStack()
    with es:
        k = K(nc, es)

        uniq = [0]

        def sbuf(stack, name, shape, dt):
            uniq[0] += 1
            return stack.enter_context(nc.sbuf_tensor("%s_%d" % (name, uniq[0]), list(shape), dt))

        def I(eng, method, reads, writes, *a, **kw):
            return k.ins(eng, lambda e: getattr(e, method)(*a, **kw), reads, writes)

        def DMA(q, out, in_, reads=(), writes=(), **kw):
            return k.dma(q, lambda e: e.dma_start(out=out, in_=in_, **kw), reads, writes)

        def MM(out, lhsT, rhs, start, stop, reads, writes):
            return k.ins("pe", lambda e: e.matmul(out, lhsT=lhsT, rhs=rhs, start=start, stop=stop), reads, writes)

        bank = [es.enter_context(nc.psum_tensor("bank%d" % i, [128, 512], F32)) for i in range(8)]
        rb = [Res("bank%d" % i, excl=True) for i in range(8)]
        ident = sbuf(es, "ident", [128, 128], F32)
        identb = sbuf(es, "identb", [128, 128], BF16)
        ones = sbuf(es, "ones", [128, 128], F32)
        rot32 = sbuf(es, "rot32", [128, 128], F32)
        rotb = sbuf(es, "rotb", [128, 128], BF16)
        j64_32 = sbuf(es, "j64_32", [64, 64], F32)
        j64b = sbuf(es, "j64b", [64, 64], BF16)
        mask32 = sbuf(es, "mask32", [64, 512], F32)
        maskb = sbuf(es, "maskb", [64, 512], BF16)
        gam = sbuf(es, "gam", [128, depth, 4, 16], F32)
        cw = sbuf(es, "cw", [128, depth, 4, 44], F32)
        subl = sbuf(es, "subl", [128, depth, 256], F32)
        lamv = sbuf(es, "lamv", [128, depth, 4, 128], F32)
        lams = sbuf(es, "lams", [128, depth, 4], F32)
        neglam = sbuf(es, "neglam", [128, depth], F32)
        r_c = Res("consts")
        r_c2 = Res("consts2")
        cres = [r_c, r_c2]

        I("pool", "memset", (), (r_c,), ident[:], 0.0)
        I("pool", "affine_select", (r_c,), (r_c,), out=ident[:], in_=ident[:], pattern=[[-1, 128]],
          compare_op=ALU.not_equal, fill=1.0, base=0, channel_multiplier=1)
        I("pool", "memset", (), (r_c2,), ones[:], 1.0)
        DMA("sp", rot32[:], c_rot, (), (r_c2,))
        DMA("sp", j64_32[:], c_j64, (), (r_c2,))
        DMA("sp", mask32[:], c_mask, (), (r_c2,))
        norm_names = ["norm_mix_pre", "norm_mix_post", "norm_ffn_pre", "norm_ffn_post"]
        for l in range(depth):
            for i, nn in enumerate(norm_names):
                DMA("sp", gam[:, l, i, :], W[nn][l].rearrange("(c p) -> p c", p=128), (), (r_c2,), allow_slow_non_contiguous=True)
            for t in range(3):
                DMA("sp", cw[:, l, t, :], W["conv_w"][l, t].rearrange("(c p) -> p c", p=128), (), (r_c2,), allow_slow_non_contiguous=True)
            DMA("sp", cw[:, l, 3, :], W["conv_b"][l].rearrange("(c p) -> p c", p=128), (), (r_c2,), allow_slow_non_contiguous=True)
        for l in range(depth):
            DMA("sp", subl[:, l, :], bass.AP(Wd["subln"], l * 256, [[0, 128], [1, 256]]), (), (r_c2,))
            for i, nn in enumerate(["lam_q1", "lam_k1", "lam_q2", "lam_k2"]):
                DMA("sp", lamv[:, l, i, :], bass.AP(Wd[nn], l * 128, [[0, 128], [1, 128]]), (), (r_c2,))
        I("dve", "tensor_copy", (r_c,), (r_c,), out=identb[:], in_=ident[:])
        I("dve", "tensor_copy", (r_c2,), (r_c2,), out=rotb[:], in_=rot32[:])
        I("dve", "tensor_copy", (r_c2,), (r_c2,), out=j64b[:], in_=j64_32[:])
        I("dve", "tensor_copy", (r_c2,), (r_c2,), out=maskb[:], in_=mask32[:])
        for l in range(depth):
            lam0 = lam_init(l)
            I("dve", "tensor_scalar", (r_c2,), (r_c2,), out=subl[:, l, :], in0=subl[:, l, :], scalar1=float(1.0 - lam0),
              scalar2=None, op0=ALU.mult)
            I("dve", "tensor_tensor", (r_c2,), (r_c2,), out=lamv[:, l, 0, :], in0=lamv[:, l, 0, :], in1=lamv[:, l, 1, :], op=ALU.mult)
            I("dve", "tensor_tensor", (r_c2,), (r_c2,), out=lamv[:, l, 2, :], in0=lamv[:, l, 2, :], in1=lamv[:, l, 3, :], op=ALU.mult)
            I("dve", "tensor_reduce", (r_c2,), (r_c2,), out=lams[:, l, 0:1], in_=lamv[:, l, 0, :], axis=AX.X, op=ALU.add)
            I("dve", "tensor_reduce", (r_c2,), (r_c2,), out=lams[:, l, 1:2], in_=lamv[:, l, 2, :], axis=AX.X, op=ALU.add)
            I("act", "activation", (r_c2,), (r_c2,), out=lams[:, l, 0:2], in_=lams[:, l, 0:2], func=AF.Exp)
            I("dve", "scalar_tensor_tensor", (r_c2,), (r_c2,), out=neglam[:, l:l + 1], in0=lams[:, l, 1:2], scalar=float(-lam0),
              in1=lams[:, l, 0:1], op0=ALU.add, op1=ALU.subtract)

        rpbpad = rpbpad_t.ap()
        zt = sbuf(es, "zt", [1, 128], F32)
        I("pool", "memset", (), (r_c2,), zt[:], 0.0)
        nrpb = depth * 8 * 15 * 31
        DMA("pool", bass.AP(rpbpad_t, 0, [[0, 1], [1, 128]]), zt[:], (r_c2,), ())
        DMA("pool", bass.AP(rpbpad_t, 128 + nrpb, [[0, 1], [1, 128]]), zt[:], (r_c2,), ())
        r_dummy = Res("dummy")
        cres.append(r_dummy)
        DMA("pool", bass.AP(rpbpad_t, 128, [[0, 1], [1, nrpb]]),
            bass.AP(Wd["rpb"], 0, [[0, 1], [1, nrpb]]), (), (r_dummy,))

        for l in range(depth):
            for g in range(20):
                DMA("pool", wbf["in", l][g], W["w_in"][l][:, g * 512:(g + 1) * 512].rearrange("(kc p) n -> p kc n", p=128), (), (r_dummy,))
            for g in range(4):
                DMA("pool", wbf["bra", l][g], W["w_br_a"][l][:, g * 512:(g + 1) * 512].rearrange("(kc p) n -> p kc n", p=128), (), (r_dummy,))
                DMA("pool", wbf["brb", l][g], W["w_br_b"][l][:, g * 512:(g + 1) * 512].rearrange("(kc p) n -> p kc n", p=128), (), (r_dummy,))
                DMA("pool", wbf["out", l][g], W["w_out"][l][:, g * 512:(g + 1) * 512].rearrange("(kc p) n -> p kc n", p=128), (), (r_dummy,))
            for g in range(22):
                DMA("pool", wbf["up", l][g], W["w_ffn_up"][l][:, g * 512:(g + 1) * 512].rearrange("(kc p) n -> p kc n", p=128), (), (r_dummy,))
            for go in range(4):
                for kr in range(4):
                    DMA("pool", wbf["dn", l][go * 4 + kr],
                        W["w_ffn_down"][l][kr * 1408:(kr + 1) * 1408, go * 512:(go + 1) * 512].rearrange("(kc p) n -> p kc n", p=128),
                        (), (r_dummy,))
        k.barrier()

        class Ring:
            def __init__(self, stack, name, n, shape, dt, plist):
                self.t = [sbuf(stack, "%s%d" % (name, i), shape, dt) for i in range(n)]
                self.r = [Res("%s%d" % (name, i)) for i in range(n)]
                plist.extend(self.r)
                self.i = 0
                self.n = n

            def next(self):
                i = self.i % self.n
                self.i += 1
                return self.t[i], self.r[i]

        evac_flip = [0]

        def rms_stats(src, r_src, nch, N, sq_ring, rstd, r_rstd, bnk):
            for c in range(nch):
                sq, r_sq = sq_ring.next()
                I("act", "activation", (r_src,), (r_sq,), out=sq[:, :N], in_=src[:, c, :N], func=AF.Square)
                MM(bank[bnk][:, :N], ones[:], sq[:, :N], c == 0, c == nch - 1, (r_sq, r_c2), (rb[bnk],))
            I("act", "activation", (rb[bnk],), (r_rstd,), out=rstd[:, :N], in_=bank[bnk][:, :N], func=AF.Sqrt,
              scale=1.0 / (nch * 128), bias=EPS)
            I("dve", "reciprocal", (r_rstd,), (r_rstd,), out=rstd[:, :N], in_=rstd[:, :N])

        def phase_in(j):
            T = jobs[j]
            pl = []
            with ExitStack() as st:
                xr = Ring(st, "p0x", 2, [128, D], F32, pl)
                xo = Ring(st, "p0o", 2, [128, 16, 128], F32, pl)
                blocks = [(W["meta_tokens"], 0, NM, 0)] + [(xin[j], b * 128, 128, NM + b * 128) for b in range(T // 128)]
                for src, r0, n, pos in blocks:
                    xt, r_xt = xr.next()
                    DMA("sp", xt[:n, :], src[r0:r0 + n, :], (), (r_xt,))
                    ot, r_ot = xo.next()
                    for q4 in range(4):
                        b = q4 % 2
                        for i in range(4):
                            c = q4 * 4 + i
                            k.ins("pe", lambda e, b=b, i=i, c=c, xt=xt, n=n: e.transpose(
                                out=bank[b][:, i * 128:i * 128 + n], in_=xt[:n, c * 128:(c + 1) * 128], identity=ident[:n, :n]),
                                (r_xt, r_c), (rb[b],))
                        src_ap = bank[b][:].rearrange("p (i t) -> p i t", i=4)[:, :, :n]
                        if q4 % 2 == 0:
                            I("act", "activation", (rb[b],), (r_ot,), out=ot[:, q4 * 4:(q4 + 1) * 4, :n], in_=src_ap, func=AF.Copy)
                        else:
                            I("dve", "tensor_copy", (rb[b],), (r_ot,), out=ot[:, q4 * 4:(q4 + 1) * 4, :n], in_=src_ap)
                    DMA("pool", SC[j]["xA"][:, :, pos:pos + n].rearrange("c p t -> p c t"), ot[:, :, :n], (r_ot,), ())
                k.barrier()
            k.release(pl)

        def phase_out(j):
            T = jobs[j]
            pl = []
            with ExitStack() as st:
                xr = Ring(st, "p6x", 2, [128, 16, 128], F32, pl)
                xo = Ring(st, "p6o", 2, [128, D], F32, pl)
                for b in range(T // 128):
                    pos = NM + b * 128
                    xt, r_xt = xr.next()
                    DMA("sp", xt[:], SC[j]["xA"][:, :, pos:pos + 128].rearrange("c p t -> p c t"), (), (r_xt,))
                    ot, r_ot = xo.next()
                    for q4 in range(4):
                        bb = q4 % 2
                        for i in range(4):
                            c = q4 * 4 + i
                            k.ins("pe", lambda e, bb=bb, i=i, c=c, xt=xt: e.transpose(
                                out=bank[bb][:, i * 128:(i + 1) * 128], in_=xt[:, c, :], identity=ident[:]),
                                (r_xt, r_c), (rb[bb],))
                        if q4 % 2 == 0:
                            I("act", "activation", (rb[bb],), (r_ot,), out=ot[:, q4 * 512:(q4 + 1) * 512], in_=bank[bb][:], func=AF.Copy)
                        else:
                            I("dve", "tensor_copy", (rb[bb],), (r_ot,), out=ot[:, q4 * 512:(q4 + 1) * 512], in_=bank[bb][:])
                    DMA("pool", yout[j][b * 128:(b + 1) * 128, :], ot[:], (r_ot,), ())
                k.barrier()
            k.release(pl)

        def pos_tiles(T):
            return [(0, NM)] + [(NM + i * 512, 512) for i in range(T // 512)]

        def phase_proj(j, l):
            T = jobs[j]
            L = T + NM
            S = SC[j]
            pl = []
            with ExitStack() as st:
                xt = sbuf(st, "p1x", [128, 16, 512], F32); r_xt = Res("p1x")
                hT = sbuf(st, "p1h", [128, 16, 512], BF16); r_hT = Res("p1h")
                rstd = sbuf(st, "p1r", [128, 512], F32); r_rstd = Res("p1r")
                cs = sbuf(st, "p1cs", [128, 2, 512], F32); r_cs = Res("p1cs")
                pl.extend([r_xt, r_hT, r_rstd, r_cs])
                sqr = Ring(st, "p1sq", 2, [128, 512], F32, pl)
                slab = Ring(st, "p1w", 4, [128, 16, 512], BF16, pl)
                stg = Ring(st, "p1st", 3, [128, 4, 512], BF16, pl)
                qraw = Ring(st, "p1q", 2, [128, 512], BF16, pl)
                t1r = Ring(st, "p1t1", 2, [128, 512], F32, pl)
                t2r = Ring(st, "p1t2", 2, [128, 512], F32, pl)
                pb = [0]

                def nextbank():
                    b = 1 + (pb[0] % 5)
                    pb[0] += 1
                    return b

                for (p0, N) in pos_tiles(T):
                    DMA("sp", xt[:, :, :N], S["xA"][:, :, p0:p0 + N].rearrange("c p t -> p c t"), (), (r_xt,))
                    DMA("sp", cs[:, :, :N], cs_in[j][:, :, p0:p0 + N].rearrange("a p t -> p a t"), (), (r_cs,))
                    rms_stats(xt, r_xt, 16, N, sqr, rstd, r_rstd, 0)
                    for c in range(16):
                        I("dve", "scalar_tensor_tensor", (r_xt, r_rstd, r_c2), (r_hT,), out=hT[:, c, :N], in0=xt[:, c, :N],
                          scalar=gam[:, l, 0, c:c + 1], in1=rstd[:, :N], op0=ALU.mult, op1=ALU.mult)
                    for g in range(20):
                        wt, r_wt = slab.next()
                        DMA("sp", wt[:], wbf["in", l][g], (), (r_wt,))
                        if g in (4, 5, 10, 11):
                            dst = S["va"] if g < 6 else S["vb"]
                            col0 = (g - 4) * 512 if g < 6 else (g - 10) * 512
                            for s0 in range(0, N, 128):
                                n = min(128, N - s0)
                                b = nextbank()
                                for kc in range(16):
                                    MM(bank[b][:n, :], hT[:, kc, s0:s0 + n], wt[:, kc, :], kc == 0, kc == 15, (r_hT, r_wt), (rb[b],))
                                so, r_so = qraw.next()
                                I("act", "activation", (rb[b],), (r_so,), out=so[:n, :], in_=bank[b][:n, :], func=AF.Copy)
                                DMA("pool", dst[p0 + s0:p0 + s0 + n, col0:col0 + 512], so[:n, :], (r_so,), ())
                            continue
                        so, r_so = stg.next()
                        for oc in range(4):
                            b = nextbank()
                            for kc in range(16):
                                MM(bank[b][:, :N], wt[:, kc, oc * 128:(oc + 1) * 128], hT[:, kc, :N], kc == 0, kc == 15, (r_hT, r_wt), (rb[b],))
                            if g < 4:
                                qr, r_qr = qraw.next()
                                I("act", "activation", (rb[b],), (r_qr,), out=qr[:, :N], in_=bank[b][:, :N], func=AF.Copy)
                                MM(bank[7][:, :N], rotb[:], qr[:, :N], True, True, (r_qr, r_c2), (rb[7],))
                                t1, r_t1 = t1r.next()
                                t2, r_t2 = t2r.next()
                                I("dve", "tensor_tensor", (rb[b], r_cs), (r_t1,), out=t1[:, :N], in0=bank[b][:, :N], in1=cs[:, 0, :N], op=ALU.mult)
                                I("dve", "tensor_tensor", (rb[7], r_cs), (r_t2,), out=t2[:, :N], in0=bank[7][:, :N], in1=cs[:, 1, :N], op=ALU.mult)
                                I("pool", "tensor_tensor", (r_t1, r_t2), (r_so,), out=so[:, oc, :N], in0=t1[:, :N], in1=t2[:, :N], op=ALU.add)
                            elif g in (6, 7):
                                I("act", "activation", (rb[b],), (r_so,), out=so[:, oc, :N], in_=bank[b][:, :N], func=AF.Copy, scale=SCALE)
                            elif g in (8, 9):
                                I("act", "activation", (rb[b],), (r_so,), out=so[:, oc, :N], in_=bank[b][:, :N], func=AF.Copy)
                            else:
                                I("act", "activation", (rb[b],), (r_so,), out=so[:, oc, :N], in_=bank[b][:, :N], func=AF.Sigmoid)
                        if g < 2:
                            dst = S["qaT"][g * 4:(g + 1) * 4]
                        elif g < 4:
                            dst = S["kaT"][(g - 2) * 4:(g - 1) * 4]
                        elif g < 8:
                            dst = S["qbT"][(g - 6) * 4:(g - 5) * 4]
                        elif g < 10:
                            dst = S["kbT"][(g - 8) * 4:(g - 7) * 4]
                        elif g < 16:
                            dst = S["gaT"][(g - 12) * 4:(g - 11) * 4]
                        else:
                            dst = S["gbT"][(g - 16) * 4:(g - 15) * 4]
                        DMA("pool", dst[:, :, p0:p0 + N].rearrange("c p t -> p c t"), so[:, :, :N], (r_so,), ())
                k.barrier()
            k.release(pl)

        def phase_diff(j, l):
            T = jobs[j]
            L = T + NM
            S = SC[j]
            nkb = (L + 127) // 128
            kbs = [(i * 128, min(128, L - i * 128)) for i in range(nkb)]
            pl = []
            with ExitStack() as st:
                vp = sbuf(st, "p2v", [128, nkb, 257], BF16); r_vp = Res("p2v")
                kT = sbuf(st, "p2k", [128, 2, L], BF16); r_kT = Res("p2k")
                qT = sbuf(st, "p2q", [128, 2, L], BF16); r_qT = Res("p2q")
                stt = sbuf(st, "p2s", [128, 2, 2, 40], F32); r_stt = Res("p2s")
                negc = sbuf(st, "p2c", [128, 2], F32); r_negc = Res("p2c")
                pl.extend([r_vp, r_kT, r_qT, r_stt, r_negc])
                sqr = Ring(st, "p2sq", 2, [128, 512], F32, pl)
                ptr = Ring(st, "p2p", 4, [128, 512], BF16, pl)
                accs = Ring(st, "p2a", 2, [128, 4, 257], F32, pl)
                otr = Ring(st, "p2o", 2, [128, 256], F32, pl)
                o2r = Ring(st, "p2o2", 2, [128, 256], F32, pl)
                obr = Ring(st, "p2ob", 2, [128, 4, 256], BF16, pl)
                oTr = Ring(st, "p2oT", 2, [128, 2, 512], BF16, pl)
                smr = Ring(st, "p2sm", 4, [128, 8], F32, pl)
                I("pool", "memset", (), (r_vp,), vp[:, :, 256:257], 1.0)
                sb_i = [0]
                for h in range(4):
                    for c in range(2):
                        DMA("sp", kT[:, c, :], S["kaT"][h * 2 + c], (), (r_kT,))
                        DMA("sp", qT[:, c, :], S["qaT"][h * 2 + c], (), (r_qT,))
                    for (k0, kn) in kbs:
                        DMA("sp", vp[:kn, k0 // 128, 0:256], S["va"][k0:k0 + kn, h * 256:(h + 1) * 256], (), (r_vp,))
                    chunks = [(i * 512, min(512, L - i * 512)) for i in range((L + 511) // 512)]
                    for c in range(2):
                        for wi, (src, r_src) in enumerate(((kT, r_kT), (qT, r_qT))):
                            for ci, (c0, cn) in enumerate(chunks):
                                sq, r_sq = sqr.next()
                                I("act", "activation", (r_src,), (r_sq,), out=sq[:, :cn], in_=src[:, c, c0:c0 + cn], func=AF.Square)
                                MM(bank[0][:, :cn], ones[:], sq[:, :cn], True, True, (r_sq, r_c2), (rb[0],))
                                I("dve", "tensor_reduce", (rb[0],), (r_stt,), out=stt[:, c, wi, ci:ci + 1], in_=bank[0][:, :cn], axis=AX.X, op=ALU.max)
                            I("dve", "tensor_reduce", (r_stt,), (r_stt,), out=stt[:, c, wi, 39:40], in_=stt[:, c, wi, 0:len(chunks)], axis=AX.X, op=ALU.max)
                        I("dve", "tensor_tensor", (r_stt,), (r_stt,), out=stt[:, c, 0, 38:39], in0=stt[:, c, 0, 39:40], in1=stt[:, c, 1, 39:40], op=ALU.mult)
                        I("act", "activation", (r_stt,), (r_stt,), out=stt[:, c, 0, 38:39], in_=stt[:, c, 0, 38:39], func=AF.Sqrt)
                        I("dve", "tensor_scalar", (r_stt,), (r_negc,), out=negc[:, c:c + 1], in0=stt[:, c, 0, 38:39], scalar1=float(-SCALE), scalar2=None, op0=ALU.mult)
                    for (q0, N) in pos_tiles(T):
                        nst = (N + 127) // 128
                        acc = []
                        for c in range(2):
                            at, r_at = accs.next()
                            for (k0, kn) in kbs:
                                sbk = 4 + (sb_i[0] % 3)
                                sb_i[0] += 1
                                MM(bank[sbk][:kn, :N], kT[:, c, k0:k0 + kn], qT[:, c, q0:q0 + N], True, True, (r_kT, r_qT), (rb[sbk],))
                                pt, r_pt = ptr.next()
                                I("act", "activation", (rb[sbk], r_negc), (r_pt,), out=pt[:kn, :N], in_=bank[sbk][:kn, :N], func=AF.Exp,
                                  scale=SCALE, bias=negc[:kn, c:c + 1])
                                for s_ in range(nst):
                                    n = min(128, N - s_ * 128)
                                    MM(bank[s_][:n, 0:257], pt[:kn, s_ * 128:s_ * 128 + n], vp[:kn, k0 // 128, :], k0 == 0, k0 == kbs[-1][0],
                                       (r_pt, r_vp), (rb[s_],))
                            for s_ in range(nst):
                                n = min(128, N - s_ * 128)
                                if (s_ + c) % 2 == 0:
                                    I("act", "activation", (rb[s_],), (r_at,), out=at[:n, s_, :], in_=bank[s_][:n, 0:257], func=AF.Copy)
                                else:
                                    I("dve", "tensor_copy", (rb[s_],), (r_at,), out=at[:n, s_, :], in_=bank[s_][:n, 0:257])
                            acc.append((at, r_at))
                        (a1, r_a1), (a2, r_a2) = acc
                        ob, r_ob = obr.next()
                        for s_ in range(nst):
                            n = min(128, N - s_ * 128)
                            sm, r_sm = smr.next()
                            I("dve", "reciprocal", (r_a1,), (r_sm,), out=sm[:n, 0:1], in_=a1[:n, s_, 256:257])
                            I("dve", "reciprocal", (r_a2,), (r_sm,), out=sm[:n, 1:2], in_=a2[:n, s_, 256:257])
                            I("dve", "tensor_tensor", (r_sm, r_c2), (r_sm,), out=sm[:n, 2:3], in0=sm[:n, 1:2], in1=neglam[:n, l:l + 1], op=ALU.mult)
                            ot, r_ot = otr.next()
                            o2, r_o2 = o2r.next()
                            I("dve", "tensor_scalar", (r_a2, r_sm), (r_ot,), out=ot[:n, :], in0=a2[:n, s_, 0:256], scalar1=sm[:n, 2:3], scalar2=None, op0=ALU.mult)
                            I("dve", "scalar_tensor_tensor", (r_a1, r_sm, r_ot), (r_o2,), out=o2[:n, :], in0=a1[:n, s_, 0:256], scalar=sm[:n, 0:1],
                              in1=ot[:n, :], op0=ALU.mult, op1=ALU.add)
                            I("act", "activation", (r_o2,), (r_ot, r_sm), out=ot[:n, :], in_=o2[:n, :], func=AF.Square, accum_out=sm[:n, 3:4])
                            I("act", "activation", (r_sm,), (r_sm,), out=sm[:n, 4:5], in_=sm[:n, 3:4], func=AF.Sqrt, scale=1.0 / 256, bias=EPS)
                            I("dve", "reciprocal", (r_sm,), (r_sm,), out=sm[:n, 5:6], in_=sm[:n, 4:5])
                            I("dve", "scalar_tensor_tensor", (r_o2, r_sm, r_c2), (r_ob,), out=ob[:n, s_, :], in0=o2[:n, :], scalar=sm[:n, 5:6],
                              in1=subl[:n, l, :], op0=ALU.mult, op1=ALU.mult)
                        oT, r_oT = oTr.next()
                        pbv = bank[7][:].bitcast(BF16)
                        for f in range(2):
                            for s_ in range(nst):
                                n = min(128, N - s_ * 128)
                                k.ins("pe", lambda e, f=f, s_=s_, n=n, ob=ob, pbv=pbv: e.transpose(
                                    out=pbv[:, f * 512 + s_ * 128:f * 512 + s_ * 128 + n], in_=ob[:n, s_, f * 128:(f + 1) * 128], identity=identb[:n, :n]),
                                    (r_ob, r_c), (rb[7],))
                        I("dve", "tensor_copy", (rb[7],), (r_oT,), out=oT[:, :, :N], in_=pbv.rearrange("p (f t) -> p f t", f=2)[:, :, :N])
                        DMA("pool", S["oaT"][h * 2:h * 2 + 2, :, q0:q0 + N].rearrange("c p t -> p c t"), oT[:, :, :N], (r_oT,), ())
                k.barrier()
            k.release(pl)

        def phase_na(j, l):
            T = jobs[j]
            L = T + NM
            S = SC[j]
            rows = T // 64
            pl = []
            with ExitStack() as st:
                hk = sbuf(st, "p3hk", [64, 8, 15, 64], BF16); r_hk = Res("p3hk")
                hk32 = sbuf(st, "p3hk32", [64, 8, 15, 64], F32); r_hk32 = Res("p3hk32")
                km = sbuf(st, "p3km", [128, 8, NM], BF16); r_km = Res("p3km")
                qm = sbuf(st, "p3qm", [128, 8, NM], BF16); r_qm = Res("p3qm")
                vm = sbuf(st, "p3vm", [NM, 8, 129], BF16); r_vm = Res("p3vm")
                pl.extend([r_hk, r_hk32, r_km, r_qm, r_vm])
                qtr = Ring(st, "p3q", 2, [128, 8, 512], BF16, pl)
                kwr = Ring(st, "p3k", 2, [128, 8, 1024], BF16, pl)
                vwr = Ring(st, "p3v", 2, [64, 16, 8, 129], BF16, pl)
                sqr = Ring(st, "p3sq", 2, [128, 512], F32, pl)
                sttr = Ring(st, "p3st", 2, [128, 64], F32, pl)
                ptr = Ring(st, "p3p", 3, [64, 512], BF16, pl)
                pmr = Ring(st, "p3pm", 3, [NM, 64], BF16, pl)
                obr = Ring(st, "p3ob", 2, [64, 8, 128], BF16, pl)
                oTr = Ring(st, "p3oT", 2, [128, 8, 512], BF16, pl)
                smr = Ring(st, "p3sm", 4, [64, 8], F32, pl)
                base = 128 + l * 8 * 15 * 31 - 48
                DMA("sp", hk32[:], bass.AP(rpbpad_t, base, [[1, 64], [465, 8], [31, 15], [1, 64]]), (), (r_hk32,))
                I("dve", "tensor_copy", (r_hk32,), (r_hk,), out=hk[:], in_=hk32[:])
                for v_ in vwr.t:
                    I("pool", "memset", (), (vwr.r[vwr.t.index(v_)],), v_[:, :, :, 128:129], 1.0)
                I("pool", "memset", (), (r_vm,), vm[:, :, 128:129], 1.0)
                DMA("sp", km[:], S["kbT"][:, :, 0:NM].rearrange("h p t -> p h t"), (), (r_km,))
                DMA("sp", qm[:], S["qbT"][:, :, 0:NM].rearrange("h p t -> p h t"), (), (r_qm,))
                DMA("sp", vm[:, :, 0:128], S["vb"][0:NM, :].rearrange("t (h d) -> t h d", h=8), (), (r_vm,))
                sb_i = [0]

                def shift_bound(parts, r_parts, stt, r_stt):
                    for wi in range(2):
                        cnt = 0
                        for (src, r_src, w) in parts[wi]:
                            for h in range(8):
                                for c0 in range(0, w, 512):
                                    cn = min(512, w - c0)
                                    sq, r_sq = sqr.next()
                                    I("act", "activation", (r_src,), (r_sq,), out=sq[:, :cn], in_=src[:, h, c0:c0 + cn], func=AF.Square)
                                    MM(bank[0][:, :cn], ones[:], sq[:, :cn], True, True, (r_sq, r_c2), (rb[0],))
                                    I("dve", "tensor_reduce", (rb[0],), (r_stt,), out=stt[:, wi * 28 + cnt:wi * 28 + cnt + 1], in_=bank[0][:, :cn], axis=AX.X, op=ALU.max)
                                    cnt += 1
                        I("dve", "tensor_reduce", (r_stt,), (r_stt,), out=stt[:, 56 + wi:57 + wi], in_=stt[:, wi * 28:wi * 28 + cnt], axis=AX.X, op=ALU.max)
                    I("dve", "tensor_tensor", (r_stt,), (r_stt,), out=stt[:, 58:59], in0=stt[:, 56:57], in1=stt[:, 57:58], op=ALU.mult)
                    I("act", "activation", (r_stt,), (r_stt,), out=stt[:, 59:60], in_=stt[:, 58:59], func=AF.Sqrt)
                    I("dve", "tensor_scalar", (r_stt,), (r_stt,), out=stt[:, 60:61], in0=stt[:, 59:60], scalar1=-1.0, scalar2=-1.0, op0=ALU.mult, op1=ALU.add)

                def attend(qsrc, qc0, nq, win, stt, r_q, r_kw, r_vw, r_stt, ob, r_ob, kw=None, vw=None):
                    for h in range(8):
                        bS = 1 + (sb_i[0] % 2)
                        bM = 3 + (sb_i[0] % 2)
                        bO = 5 + (sb_i[0] % 2)
                        sb_i[0] += 1
                        nw = 0
                        if win is not None:
                            w0, dr0 = win
                            nw = 8
                            MM(bank[bS][0:64, 0:512], identb[0:64, 0:64], maskb[:, :], True, False, (r_c, r_c2), (rb[bS],))
                            for i in range(8):
                                MM(bank[bS][0:64, i * 64:(i + 1) * 64], kw[:, h, (w0 + i) * 64:(w0 + i + 1) * 64], qsrc[:, h, qc0:qc0 + nq],
                                   False, False, (r_kw, r_q), (rb[bS],))
                            for i in range(8):
                                MM(bank[bS][0:64, i * 64:(i + 1) * 64], hk[:, h, dr0 + i, :], j64b[:, :], False, i == 7, (r_hk, r_c2), (rb[bS],))
                        MM(bank[bM][0:NM, 0:nq], km[:, h, :], qsrc[:, h, qc0:qc0 + nq], True, True, (r_km, r_q), (rb[bM],))
                        pm, r_pm = pmr.next()
                        I("act", "activation", (rb[bM], r_stt), (r_pm,), out=pm[:, :nq], in_=bank[bM][0:NM, 0:nq], func=AF.Exp, bias=stt[0:NM, 60:61])
                        if nw:
                            pt, r_pt = ptr.next()
                            I("act", "activation", (rb[bS], r_stt), (r_pt,), out=pt[:, :], in_=bank[bS][0:64, 0:512], func=AF.Exp, bias=stt[0:64, 60:61])
                            for i in range(8):
                                MM(bank[bO][0:nq, 0:129], pt[:, i * 64:(i + 1) * 64], vw[:, w0 + i, h, :], i == 0, False, (r_pt, r_vw), (rb[bO],))
                        MM(bank[bO][0:nq, 0:129], pm[:, :nq], vm[:, h, :], nw == 0, True, (r_pm, r_vm), (rb[bO],))
                        sm, r_sm = smr.next()
                        I("dve", "reciprocal", (rb[bO],), (r_sm,), out=sm[:nq, 0:1], in_=bank[bO][0:nq, 128:129])
                        I("dve", "tensor_scalar", (rb[bO], r_sm), (r_ob,), out=ob[:nq, h, :], in0=bank[bO][0:nq, 0:128], scalar1=sm[:nq, 0:1], scalar2=None, op0=ALU.mult)

                def flush(ob, r_ob, nq, oT, r_oT, col0):
                    pbv = bank[7][:].bitcast(BF16)
                    for h in range(8):
                        k.ins("pe", lambda e, h=h, ob=ob, nq=nq, pbv=pbv: e.transpose(
                            out=pbv[:, h * 64:h * 64 + nq], in_=ob[:nq, h, :], identity=identb[:nq, :nq]), (r_ob, r_c), (rb[7],))
                    I("dve", "tensor_copy", (rb[7],), (r_oT,), out=oT[:, :, col0:col0 + nq], in_=pbv[:, 0:512].rearrange("p (h t) -> p h t", h=8)[:, :, :nq])

                stt, r_stt = sttr.next()
                shift_bound([[(qm, r_qm, NM)], [(km, r_km, NM)]], None, stt, r_stt)
                ob, r_ob = obr.next()
                oT, r_oT = oTr.next()
                attend(qm, 0, NM, None, stt, r_qm, None, None, r_stt, ob, r_ob)
                flush(ob, r_ob, NM, oT, r_oT, 0)
                DMA("pool", S["obT"][:, :, 0:NM].rearrange("h p t -> p h t"), oT[:, :, 0:NM], (r_oT,), ())
                for ti in range(T // 512):
                    r0 = ti * 8
                    wr0 = max(0, min(r0 - 4, rows - 16))
                    wr0 = min(wr0, max(0, rows - 16))
                    nwr = min(16, rows - wr0)
                    qt, r_qt = qtr.next()
                    kw, r_kw = kwr.next()
                    vw, r_vw = vwr.next()
                    p0 = NM + ti * 512
                    DMA("sp", qt[:], S["qbT"][:, :, p0:p0 + 512].rearrange("h p t -> p h t"), (), (r_qt,))
                    DMA("sp", kw[:, :, 0:nwr * 64], S["kbT"][:, :, NM + wr0 * 64:NM + (wr0 + nwr) * 64].rearrange("h p t -> p h t"), (), (r_kw,))
                    for rr in range(nwr):
                        DMA("sp", vw[:, rr, :, 0:128], S["vb"][NM + (wr0 + rr) * 64:NM + (wr0 + rr + 1) * 64, :].rearrange("t (h d) -> t h d", h=8), (), (r_vw,))
                    stt, r_stt = sttr.next()
                    shift_bound([[(qt, r_qt, 512)], [(kw, r_kw, nwr * 64), (km, r_km, NM)]], None, stt, r_stt)
                    oT, r_oT = oTr.next()
                    for jr in range(8):
                        r = r0 + jr
                        rs = max(0, min(r - 4, rows - 8))
                        dr0 = rs - r + 7
                        ob, r_ob = obr.next()
                        attend(qt, jr * 64, 64, (rs - wr0, dr0), stt, r_qt, r_kw, r_vw, r_stt, ob, r_ob, kw=kw, vw=vw)
                        flush(ob, r_ob, 64, oT, r_oT, jr * 64)
                    DMA("pool", S["obT"][:, :, p0:p0 + 512].rearrange("h p t -> p h t"), oT[:], (r_oT,), ())
                k.barrier()
            k.release(pl)

        def post_norm_residual(yT, r_yT, xt, r_xt, N, l, gi, rstd, r_rstd, tmpr, c_lo=0, x_off=0):
            for c in range(16):
                tm, r_tm = tmpr.next()
                I("dve", "scalar_tensor_tensor", (r_yT, r_rstd, r_c2), (r_tm,), out=tm[:, :N], in0=yT[:, c, :N], scalar=gam[:, l, gi, c:c + 1],
                  in1=rstd[:, :N], op0=ALU.mult, op1=ALU.mult)
                I("pool", "tensor_tensor", (r_tm, r_xt), (r_yT,), out=yT[:, c, :N], in0=tm[:, :N], in1=xt[:, c, x_off:x_off + N], op=ALU.add)

        def phase_mix(j, l):
            T = jobs[j]
            L = T + NM
            S = SC[j]
            pl = []
            with ExitStack() as st:
                xt = sbuf(st, "p4x", [128, 16, 512], F32); r_xt = Res("p4x")
                oa = sbuf(st, "p4oa", [128, 8, 512], BF16); r_oa = Res("p4oa")
                obt = sbuf(st, "p4ob", [128, 8, 512], BF16); r_obt = Res("p4ob")
                mix = sbuf(st, "p4m", [128, 16, 512], BF16); r_mix = Res("p4m")
                yT = sbuf(st, "p4y", [128, 16, 512], F32); r_yT = Res("p4y")
                rstd = sbuf(st, "p4r", [128, 512], F32); r_rstd = Res("p4r")
                pl.extend([r_xt, r_oa, r_obt, r_mix, r_yT, r_rstd])
                gar = Ring(st, "p4ga", 2, [128, 4, 512], BF16, pl)
                gbr = Ring(st, "p4gb", 2, [128, 4, 512], BF16, pl)
                slab = Ring(st, "p4w", 3, [128, 16, 512], BF16, pl)
                sqr = Ring(st, "p4sq", 2, [128, 512], F32, pl)
                t1r = Ring(st, "p4t1", 2, [128, 512], F32, pl)
                t2r = Ring(st, "p4t2", 2, [128, 512], F32, pl)
                pb = [0]
                for (p0, N) in pos_tiles(T):
                    DMA("sp", oa[:, :, :N], S["oaT"][:, :, p0:p0 + N].rearrange("c p t -> p c t"), (), (r_oa,))
                    DMA("sp", obt[:, :, :N], S["obT"][:, :, p0:p0 + N].rearrange("c p t -> p c t"), (), (r_obt,))
                    DMA("sp", xt[:, :, :N], S["xA"][:, :, p0:p0 + N].rearrange("c p t -> p c t"), (), (r_xt,))
                    for go in range(4):
                        wa, r_wa = slab.next()
                        DMA("sp", wa[:, 0:8, :], wbf["bra", l][go], (), (r_wa,))
                        DMA("sp", wa[:, 8:16, :], wbf["brb", l][go], (), (r_wa,))
                        ga, r_ga = gar.next()
                        gb, r_gb = gbr.next()
                        DMA("sp", ga[:, :, :N], S["gaT"][go * 4:(go + 1) * 4, :, p0:p0 + N].rearrange("c p t -> p c t"), (), (r_ga,))
                        DMA("sp", gb[:, :, :N], S["gbT"][go * 4:(go + 1) * 4, :, p0:p0 + N].rearrange("c p t -> p c t"), (), (r_gb,))
                        for oc in range(4):
                            ba = 1 + (pb[0] % 2)
                            bb = 3 + (pb[0] % 2)
                            pb[0] += 1
                            for kc in range(8):
                                MM(bank[ba][:, :N], wa[:, kc, oc * 128:(oc + 1) * 128], oa[:, kc, :N], kc == 0, kc == 7, (r_wa, r_oa), (rb[ba],))
                            for kc in range(8):
                                MM(bank[bb][:, :N], wa[:, 8 + kc, oc * 128:(oc + 1) * 128], obt[:, kc, :N], kc == 0, kc == 7, (r_wa, r_obt), (rb[bb],))
                            t1, r_t1 = t1r.next()
                            t2, r_t2 = t2r.next()
                            I("dve", "tensor_tensor", (rb[ba], r_ga), (r_t1,), out=t1[:, :N], in0=bank[ba][:, :N], in1=ga[:, oc, :N], op=ALU.mult)
                            I("dve", "tensor_tensor", (rb[bb], r_gb), (r_t2,), out=t2[:, :N], in0=bank[bb][:, :N], in1=gb[:, oc, :N], op=ALU.mult)
                            I("pool", "tensor_tensor", (r_t1, r_t2), (r_mix,), out=mix[:, go * 4 + oc, :N], in0=t1[:, :N], in1=t2[:, :N], op=ALU.add)
                    for go in range(4):
                        wo, r_wo = slab.next()
                        DMA("sp", wo[:], wbf["out", l][go], (), (r_wo,))
                        for oc in range(4):
                            c = go * 4 + oc
                            b = 5 + (pb[0] % 2)
                            pb[0] += 1
                            for kc in range(16):
                                MM(bank[b][:, :N], wo[:, kc, oc * 128:(oc + 1) * 128], mix[:, kc, :N], kc == 0, kc == 15, (r_wo, r_mix), (rb[b],))
                            I("dve", "tensor_copy", (rb[b],), (r_yT,), out=yT[:, c, :N], in_=bank[b][:, :N])
                            sq, r_sq = sqr.next()
                            I("act", "activation", (rb[b],), (r_sq,), out=sq[:, :N], in_=bank[b][:, :N], func=AF.Square)
                            MM(bank[0][:, :N], ones[:], sq[:, :N], c == 0, c == 15, (r_sq, r_c2), (rb[0],))
                    I("act", "activation", (rb[0],), (r_rstd,), out=rstd[:, :N], in_=bank[0][:, :N], func=AF.Sqrt, scale=1.0 / D, bias=EPS)
                    I("dve", "reciprocal", (r_rstd,), (r_rstd,), out=rstd[:, :N], in_=rstd[:, :N])
                    post_norm_residual(yT, r_yT, xt, r_xt, N, l, 1, rstd, r_rstd, t1r)
                    DMA("pool", S["xB"][:, :, p0:p0 + N].rearrange("c p t -> p c t"), yT[:, :, :N], (r_yT,), ())
                k.barrier()
            k.release(pl)

        def phase_ffn(j, l):
            T = jobs[j]
            L = T + NM
            S = SC[j]
            pl = []
            with ExitStack() as st:
                xt = sbuf(st, "p5x", [128, 16, 512], F32); r_xt = Res("p5x")
                hT = sbuf(st, "p5h", [128, 16, 512], BF16); r_hT = Res("p5h")
                act = sbuf(st, "p5a", [128, 44, 512], BF16); r_act = Res("p5a")
                fT = sbuf(st, "p5f", [128, 16, 512], F32); r_fT = Res("p5f")
                rstd = sbuf(st, "p5r", [128, 512], F32); r_rstd = Res("p5r")
                pl.extend([r_xt, r_hT, r_act, r_fT, r_rstd])
                slab = Ring(st, "p5w", 3, [128, 16, 512], BF16, pl)
                sqr = Ring(st, "p5sq", 2, [128, 512], F32, pl)
                a1r = Ring(st, "p5a1", 2, [128, 512], F32, pl)
                a2r = Ring(st, "p5a2", 2, [128, 512], F32, pl)
                pb = [0]
                s = 0
                while s < L:
                    nout = min(510, L - s)
                    Nc = nout + 2
                    lo = s - 1
                    c_lo = 1 if lo < 0 else 0
                    c_hi = Nc - 1 if lo + Nc > L else Nc
                    if c_lo > 0 or c_hi < Nc:
                        I("pool", "memset", (), (r_xt,), xt[:, :, :Nc], 0.0)
                    DMA("sp", xt[:, :, c_lo:c_hi], S["xB"][:, :, lo + c_lo:lo + c_hi].rearrange("c p t -> p c t"), (), (r_xt,))
                    rms_stats(xt, r_xt, 16, Nc, sqr, rstd, r_rstd, 0)
                    for c in range(16):
                        I("dve", "scalar_tensor_tensor", (r_xt, r_rstd, r_c2), (r_hT,), out=hT[:, c, :Nc], in0=xt[:, c, :Nc],
                          scalar=gam[:, l, 2, c:c + 1], in1=rstd[:, :Nc], op0=ALU.mult, op1=ALU.mult)
                    for g in range(11):
                        wg, r_wg = slab.next()
                        DMA("sp", wg[:], wbf["up", l][g], (), (r_wg,))
                        wv, r_wv = slab.next()
                        DMA("sp", wv[:], wbf["up", l][11 + g], (), (r_wv,))
                        for oc in range(4):
                            cc = g * 4 + oc
                            bg = 1 + (pb[0] % 2)
                            bv = 3 + (pb[0] % 2)
                            pb[0] += 1
                            for kc in range(16):
                                MM(bank[bg][:, :Nc], wg[:, kc, oc * 128:(oc + 1) * 128], hT[:, kc, :Nc], kc == 0, kc == 15, (r_wg, r_hT), (rb[bg],))
                            for kc in range(16):
                                MM(bank[bv][:, :Nc], wv[:, kc, oc * 128:(oc + 1) * 128], hT[:, kc, :Nc], kc == 0, kc == 15, (r_wv, r_hT), (rb[bv],))
                            a1, r_a1 = a1r.next()
                            a2, r_a2 = a2r.next()
                            I("dve", "tensor_scalar", (rb[bg], r_c2), (r_a1,), out=a1[:, :nout], in0=bank[bg][:, 1:1 + nout], scalar1=cw[:, l, 1, cc:cc + 1],
                              scalar2=cw[:, l, 3, cc:cc + 1], op0=ALU.mult, op1=ALU.add)
                            I("dve", "scalar_tensor_tensor", (rb[bg], r_a1, r_c2), (r_a2,), out=a2[:, :nout], in0=bank[bg][:, 0:nout], scalar=cw[:, l, 0, cc:cc + 1],
                              in1=a1[:, :nout], op0=ALU.mult, op1=ALU.add)
                            I("dve", "scalar_tensor_tensor", (rb[bg], r_a2, r_c2), (r_a1,), out=a1[:, :nout], in0=bank[bg][:, 2:2 + nout], scalar=cw[:, l, 2, cc:cc + 1],
                              in1=a2[:, :nout], op0=ALU.mult, op1=ALU.add)
                            I("act", "activation", (r_a1,), (r_a2,), out=a2[:, :nout], in_=a1[:, :nout], func=AF.Gelu_apprx_tanh)
                            I("dve", "tensor_tensor", (rb[bv], r_a2), (r_act,), out=act[:, cc, :nout], in0=bank[bv][:, 1:1 + nout], in1=a2[:, :nout], op=ALU.mult)
                    for go in range(4):
                        for kr in range(4):
                            wd, r_wd = slab.next()
                            DMA("sp", wd[:, 0:11, :], wbf["dn", l][go * 4 + kr], (), (r_wd,))
                            for oc in range(4):
                                for kc in range(11):
                                    MM(bank[4 + oc][:, :nout], wd[:, kc, oc * 128:(oc + 1) * 128], act[:, kr * 11 + kc, :nout],
                                       kr == 0 and kc == 0, kr == 3 and kc == 10, (r_wd, r_act), (rb[4 + oc],))
                        for oc in range(4):
                            c = go * 4 + oc
                            b = 4 + oc
                            I("dve", "tensor_copy", (rb[b],), (r_fT,), out=fT[:, c, :nout], in_=bank[b][:, :nout])
                            sq, r_sq = sqr.next()
                            I("act", "activation", (rb[b],), (r_sq,), out=sq[:, :nout], in_=bank[b][:, :nout], func=AF.Square)
                            MM(bank[0][:, :nout], ones[:], sq[:, :nout], c == 0, c == 15, (r_sq, r_c2), (rb[0],))
                    I("act", "activation", (rb[0],), (r_rstd,), out=rstd[:, :nout], in_=bank[0][:, :nout], func=AF.Sqrt, scale=1.0 / D, bias=EPS)
                    I("dve", "reciprocal", (r_rstd,), (r_rstd,), out=rstd[:, :nout], in_=rstd[:, :nout])
                    post_norm_residual(fT, r_fT, xt, r_xt, nout, l, 3, rstd, r_rstd, a1r, x_off=1)
                    DMA("pool", S["xA"][:, :, s:s + nout].rearrange("c p t -> p c t"), fT[:, :, :nout], (r_fT,), ())
                    s += nout
                k.barrier()
            k.release(pl)

        def on(name, l=0):
            return phases is None or name in phases or (name, l) in phases
        for j in range(len(jobs)):
            if on("in"):
                phase_in(j)
            for l in range(depth):
                if on("proj", l):
                    phase_proj(j, l)
                if on("diff", l):
                    phase_diff(j, l)
                if on("na", l):
                    phase_na(j, l)
                if on("mix", l):
                    phase_mix(j, l)
                if on("ffn", l):
                    phase_ffn(j, l)
            if on("out"):
                phase_out(j)
        k.barrier()
        k.emit()
    return nc


def host_consts(jobs):
    c = {}
    rot = np.zeros((128, 128), np.float32)
    for m in range(64):
        rot[m + 64, m] = -1.0
    for m in range(64, 128):
        rot[m - 64, m] = 1.0
    c["c_rot"] = rot
    c["c_j64"] = np.ascontiguousarray(np.eye(64, dtype=np.float32)[::-1])
    cstart = np.clip(np.arange(64) - 8, 0, 48)
    cp = np.arange(64)[:, None]
    cq = np.arange(64)[None, :]
    inwin = (cp >= cstart[None, :]) & (cp < cstart[None, :] + 16)
    m = np.where(inwin, 0.0, NEG).astype(np.float32)
    c["c_mask"] = np.ascontiguousarray(np.tile(m, (1, 8)))
    for j, T in enumerate(jobs):
        L = T + NM
        inv = (1.0 / (np.float32(10000.0) ** (np.arange(0, 128, 2, dtype=np.float32) / np.float32(128)))).astype(np.float32)
        ang = (np.arange(L, dtype=np.float32)[:, None] * inv[None, :]).astype(np.float32)
        cos = np.cos(ang).astype(np.float32).T
        sin = np.sin(ang).astype(np.float32).T
        c["cs%d" % j] = np.ascontiguousarray(np.stack([np.concatenate([cos, cos], 0), np.concatenate([sin, sin], 0)], 0))
    return c


_CACHE = {}


def kernel(**inputs):
    x_prompt = np.asarray(inputs["x_prompt"], np.float32)
    x_sample = np.asarray(inputs["x_sample"], np.float32)
    jobs = (x_sample.shape[1], x_prompt.shape[1])
    key = jobs
    if key not in _CACHE:
        _CACHE[key] = build_program(list(jobs), depth=2)
    nc = _CACHE[key]
    consts = host_consts(jobs)
    wts = {n: np.ascontiguousarray(np.asarray(inputs[n], np.float32)) for n, _ in WSPEC}
    in_maps = []
    for c in range(8):
        b = c % 4
        m = dict(wts)
        m.update(consts)
        m["x0"] = np.ascontiguousarray(x_sample[b])
        m["x1"] = np.ascontiguousarray(x_prompt[b])
        in_maps.append(m)
    res = run_bass_kernel_spmd(nc, in_maps, core_ids=list(range(8)))
    y_s = np.stack([res.results[b]["y0"] for b in range(4)], 0).astype(np.float32)
    y_p = np.stack([res.results[b]["y1"] for b in range(4)], 0).astype(np.float32)
    return (y_p, y_s)
```

```python
import math
import numpy as np
from contextlib import ExitStack
import concourse.bass as bass
import concourse.mybir as mybir
from concourse.bass_utils import run_bass_kernel_spmd

F32 = mybir.dt.float32
BF16 = mybir.dt.bfloat16
AF = mybir.ActivationFunctionType
ALU = mybir.AluOpType
AX = mybir.AxisListType

D = 2048
DIN = 10240
DFF = 5632
NM = 16
EPS = 1e-6
SCALE = 128.0 ** -0.5
NEG = -30000.0
ENGS = ("pe", "act", "dve", "pool", "sp")


class Res:
    __slots__ = ("name", "w", "r", "fill", "drain", "excl")

    def __init__(self, name, excl=False):
        self.name = name
        self.excl = excl
        self.w = None
        self.r = []
        self.fill = None
        self.drain = None


class SemSlot:
    __slots__ = ("sem", "cnt", "idx")

    def __init__(self, sem, idx):
        self.sem = sem
        self.cnt = 0
        self.idx = idx


class K:
    def __init__(self, nc, es, n_dma_sems=90):
        self.nc = nc
        self.recs = {e: [] for e in ENGS}
        self.seen = {e: {} for e in ENGS}
        self.esem = {}
        for e in ("pe", "act", "dve", "pool"):
            self.esem[e] = es.enter_context(nc.semaphore("es_" + e))
        self.slots = [SemSlot(es.enter_context(nc.semaphore("ds%d" % i)), i) for i in range(n_dma_sems)]
        self.free_slots = {"hw": list(self.slots[:n_dma_sems // 2]), "sw": list(self.slots[n_dma_sems // 2:])}
        self.kind = {}
        self.nins = 0

    def get_slot(self, q):
        kind = "sw" if q == "pool" else "hw"
        s = self.free_slots[kind].pop()
        self.kind[s.idx] = kind
        return s

    def release(self, res_list):
        for r in res_list:
            for s in (r.fill, r.drain):
                if s is not None:
                    self.free_slots[self.kind[s.idx]].append(s)
            r.fill = None
            r.drain = None

    def _collect(self, eng, reads, writes):
        deps = []
        for r in reads:
            if r.w is not None:
                deps.append(r.w)
        for w in writes:
            if w.w is not None:
                deps.append(w.w)
            deps.extend(w.r)
        waits = []
        seen = self.seen[eng]
        for d in deps:
            if d[0] == "E":
                x, val = d[1], d[2]
                if x == eng and eng in ("pe", "sp"):
                    continue
                key = ("E", x)
            else:
                val = d[2]
                key = ("S", d[1].idx)
            if seen.get(key, -1) >= val:
                continue
            seen[key] = val
            waits.append(d)
        return waits

    def ins(self, eng, fn, reads=(), writes=()):
        if any(r.excl for r in reads):
            writes = tuple(writes) + tuple(r for r in reads if r.excl)
            reads = tuple(r for r in reads if not r.excl)
        waits = self._collect(eng, reads, writes)
        idx = len(self.recs[eng])
        self.recs[eng].append({"fn": fn, "waits": waits, "signal": False, "dma": None})
        ev = ("E", eng, idx)
        for r in reads:
            r.r.append(ev)
        for w in writes:
            w.w = ev
            w.r = []
        self.nins += 1
        return ev

    def dma(self, q, fn, reads=(), writes=()):
        waits = self._collect(q, reads, writes)
        if writes:
            tgt = writes[0]
            if tgt.fill is None:
                tgt.fill = self.get_slot(q)
            slot = tgt.fill
        else:
            tgt = reads[0]
            if tgt.drain is None:
                tgt.drain = self.get_slot(q)
            slot = tgt.drain
        slot.cnt += 16
        ev = ("S", slot, slot.cnt)
        self.recs[q].append({"fn": fn, "waits": waits, "signal": False, "dma": slot})
        for r in reads:
            r.r.append(ev)
        for w in writes:
            w.w = ev
            w.r = []
        self.nins += 1
        return ev

    def barrier(self):
        last = {}
        for e in ("pe", "act", "dve", "pool"):
            for j in range(len(self.recs[e]) - 1, -1, -1):
                rec = self.recs[e][j]
                if rec["dma"] is None and rec["fn"] is not None:
                    last[e] = j
                    break
        for e in ENGS:
            waits = []
            seen = self.seen[e]
            for x, j in last.items():
                if x == e:
                    continue
                key = ("E", x)
                if seen.get(key, -1) >= j:
                    continue
                seen[key] = j
                waits.append(("E", x, j))
            for s in self.slots:
                if s.cnt > 0:
                    key = ("S", s.idx)
                    if seen.get(key, -1) >= s.cnt:
                        continue
                    seen[key] = s.cnt
                    waits.append(("S", s, s.cnt))
            if waits:
                self.recs[e].append({"fn": None, "waits": waits, "signal": False, "dma": None})

    def emit(self):
        nc = self.nc
        recs = self.recs
        for e in ENGS:
            for rec in recs[e]:
                for d in rec["waits"]:
                    if d[0] == "E":
                        recs[d[1]][d[2]]["signal"] = True
        val = {}
        for e in ("pe", "act", "dve", "pool"):
            c = 0
            for j, rec in enumerate(recs[e]):
                if rec["signal"]:
                    c += 1
                    val[(e, j)] = c
        esem = self.esem

        def run(e, eng):
            for rec in recs[e]:
                for d in rec["waits"]:
                    if d[0] == "E":
                        eng.wait_ge(esem[d[1]], val[(d[1], d[2])])
                    else:
                        eng.wait_ge(d[1].sem, d[2])
                if rec["fn"] is None:
                    continue
                ins = rec["fn"](eng)
                if rec["dma"] is not None:
                    ins.then_inc(rec["dma"].sem, 16)
                elif rec["signal"]:
                    ins.then_inc(esem[e], 1)

        with nc.Block() as block:
            @block.tensor
            def _(eng):
                run("pe", eng)

            @block.scalar
            def _(eng):
                run("act", eng)

            @block.vector
            def _(eng):
                run("dve", eng)

            @block.gpsimd
            def _(eng):
                run("pool", eng)

            @block.sync
            def _(eng):
                run("sp", eng)


WSPEC = [
    ("meta_tokens", (NM, D)), ("norm_mix_pre", (2, D)), ("w_in", (2, D, DIN)),
    ("lam_q1", (2, 128)), ("lam_k1", (2, 128)), ("lam_q2", (2, 128)), ("lam_k2", (2, 128)),
    ("subln", (2, 256)), ("rpb", (2, 8, 15, 31)), ("w_br_a", (2, 1024, D)), ("w_br_b", (2, 1024, D)),
    ("w_out", (2, D, D)), ("norm_mix_post", (2, D)), ("norm_ffn_pre", (2, D)),
    ("w_ffn_up", (2, D, 2 * DFF)), ("conv_w", (2, 3, DFF)), ("conv_b", (2, DFF)),
    ("w_ffn_down", (2, DFF, D)), ("norm_ffn_post", (2, D)),
]


def lam_init(l):
    return 0.8 - 0.6 * math.exp(-0.3 * l)


def build_program(jobs, depth=2, phases=None, dbg=()):
    nc = bass.Bass("TRN2", target_bir_lowering=False)
    Wd = {n: nc.dram_tensor(n, list(s), F32, kind="ExternalInput") for n, s in WSPEC}
    W = {n: t.ap() for n, t in Wd.items()}
    xin = [nc.dram_tensor("x%d" % j, [T, D], F32, kind="ExternalInput").ap() for j, T in enumerate(jobs)]
    yout = [nc.dram_tensor("y%d" % j, [T, D], F32, kind="ExternalOutput").ap() for j, T in enumerate(jobs)]
    cs_in = [nc.dram_tensor("cs%d" % j, [2, 128, T + NM], F32, kind="ExternalInput").ap() for j, T in enumerate(jobs)]
    c_rot = nc.dram_tensor("c_rot", [128, 128], F32, kind="ExternalInput").ap()
    c_j64 = nc.dram_tensor("c_j64", [64, 64], F32, kind="ExternalInput").ap()
    c_mask = nc.dram_tensor("c_mask", [64, 512], F32, kind="ExternalInput").ap()

    wbf = {}
    for l in range(depth):
        wbf["in", l] = nc.dram_tensor("wb_in%d" % l, [20, 128, 16, 512], BF16).ap()
        wbf["bra", l] = nc.dram_tensor("wb_bra%d" % l, [4, 128, 8, 512], BF16).ap()
        wbf["brb", l] = nc.dram_tensor("wb_brb%d" % l, [4, 128, 8, 512], BF16).ap()
        wbf["out", l] = nc.dram_tensor("wb_out%d" % l, [4, 128, 16, 512], BF16).ap()
        wbf["up", l] = nc.dram_tensor("wb_up%d" % l, [22, 128, 16, 512], BF16).ap()
        wbf["dn", l] = nc.dram_tensor("wb_dn%d" % l, [16, 128, 11, 512], BF16).ap()
    rpbpad_t = nc.dram_tensor("rpbpad", [depth * 8 * 15 * 31 + 256], F32)

    SC = []
    for j, T in enumerate(jobs):
        L = T + NM
        s = {}
        def dk(n):
            return "ExternalOutput" if n in dbg else "Internal"
        s["xA"] = nc.dram_tensor("xA%d" % j, [16, 128, L], F32, kind=dk("xA")).ap()
        s["xB"] = nc.dram_tensor("xB%d" % j, [16, 128, L], F32, kind=dk("xB")).ap()
        for n in ("qaT", "kaT", "qbT", "kbT", "oaT", "obT"):
            s[n] = nc.dram_tensor("%s%d" % (n, j), [8, 128, L], BF16, kind=dk(n)).ap()
        for n in ("gaT", "gbT"):
            s[n] = nc.dram_tensor("%s%d" % (n, j), [16, 128, L], BF16, kind=dk(n)).ap()
        for n in ("va", "vb"):
            s[n] = nc.dram_tensor("%s%d" % (n, j), [L, 1024], BF16, kind=dk(n)).ap()
        SC.append(s)

    es = ExitStack()
    with es:
        k = K(nc, es)

        uniq = [0]

        def sbuf(stack, name, shape, dt):
            uniq[0] += 1
            return stack.enter_context(nc.sbuf_tensor("%s_%d" % (name, uniq[0]), list(shape), dt))

        def I(eng, method, reads, writes, *a, **kw):
            return k.ins(eng, lambda e: getattr(e, method)(*a, **kw), reads, writes)

        def DMA(q, out, in_, reads=(), writes=(), **kw):
            return k.dma(q, lambda e: e.dma_start(out=out, in_=in_, **kw), reads, writes)

        def MM(out, lhsT, rhs, start, stop, reads, writes):
            return k.ins("pe", lambda e: e.matmul(out, lhsT=lhsT, rhs=rhs, start=start, stop=stop), reads, writes)

        bank = [es.enter_context(nc.psum_tensor("bank%d" % i, [128, 512], F32)) for i in range(8)]
        rb = [Res("bank%d" % i, excl=True) for i in range(8)]
        ident = sbuf(es, "ident", [128, 128], F32)
        identb = sbuf(es, "identb", [128, 128], BF16)
        ones = sbuf(es, "ones", [128, 128], F32)
        rot32 = sbuf(es, "rot32", [128, 128], F32)
        rotb = sbuf(es, "rotb", [128, 128], BF16)
        j64_32 = sbuf(es, "j64_32", [64, 64], F32)
        j64b = sbuf(es, "j64b", [64, 64], BF16)
        mask32 = sbuf(es, "mask32", [64, 512], F32)
        maskb = sbuf(es, "maskb", [64, 512], BF16)
        gam = sbuf(es, "gam", [128, depth, 4, 16], F32)
        cw = sbuf(es, "cw", [128, depth, 4, 44], F32)
        subl = sbuf(es, "subl", [128, depth, 256], F32)
        lamv = sbuf(es, "lamv", [128, depth, 4, 128], F32)
        lams = sbuf(es, "lams", [128, depth, 4], F32)
        neglam = sbuf(es, "neglam", [128, depth], F32)
        r_c = Res("consts")
        r_c2 = Res("consts2")
        cres = [r_c, r_c2]

        I("pool", "memset", (), (r_c,), ident[:], 0.0)
        I("pool", "affine_select", (r_c,), (r_c,), out=ident[:], in_=ident[:], pattern=[[-1, 128]],
          compare_op=ALU.not_equal, fill=1.0, base=0, channel_multiplier=1)
        I("pool", "memset", (), (r_c2,), ones[:], 1.0)
        DMA("sp", rot32[:], c_rot, (), (r_c2,))
        DMA("sp", j64_32[:], c_j64, (), (r_c2,))
        DMA("sp", mask32[:], c_mask, (), (r_c2,))
        norm_names = ["norm_mix_pre", "norm_mix_post", "norm_ffn_pre", "norm_ffn_post"]
        for l in range(depth):
            for i, nn in enumerate(norm_names):
                DMA("sp", gam[:, l, i, :], W[nn][l].rearrange("(c p) -> p c", p=128), (), (r_c2,), allow_slow_non_contiguous=True)
            for t in range(3):
                DMA("sp", cw[:, l, t, :], W["conv_w"][l, t].rearrange("(c p) -> p c", p=128), (), (r_c2,), allow_slow_non_contiguous=True)
            DMA("sp", cw[:, l, 3, :], W["conv_b"][l].rearrange("(c p) -> p c", p=128), (), (r_c2,), allow_slow_non_contiguous=True)
        for l in range(depth):
            DMA("sp", subl[:, l, :], bass.AP(Wd["subln"], l * 256, [[0, 128], [1, 256]]), (), (r_c2,))
            for i, nn in enumerate(["lam_q1", "lam_k1", "lam_q2", "lam_k2"]):
                DMA("sp", lamv[:, l, i, :], bass.AP(Wd[nn], l * 128, [[0, 128], [1, 128]]), (), (r_c2,))
        I("dve", "tensor_copy", (r_c,), (r_c,), out=identb[:], in_=ident[:])
        I("dve", "tensor_copy", (r_c2,), (r_c2,), out=rotb[:], in_=rot32[:])
        I("dve", "tensor_copy", (r_c2,), (r_c2,), out=j64b[:], in_=j64_32[:])
        I("dve", "tensor_copy", (r_c2,), (r_c2,), out=maskb[:], in_=mask32[:])
        for l in range(depth):
            lam0 = lam_init(l)
            I("dve", "tensor_scalar", (r_c2,), (r_c2,), out=subl[:, l, :], in0=subl[:, l, :], scalar1=float(1.0 - lam0),
              scalar2=None, op0=ALU.mult)
            I("dve", "tensor_tensor", (r_c2,), (r_c2,), out=lamv[:, l, 0, :], in0=lamv[:, l, 0, :], in1=lamv[:, l, 1, :], op=ALU.mult)
            I("dve", "tensor_tensor", (r_c2,), (r_c2,), out=lamv[:, l, 2, :], in0=lamv[:, l, 2, :], in1=lamv[:, l, 3, :], op=ALU.mult)
            I("dve", "tensor_reduce", (r_c2,), (r_c2,), out=lams[:, l, 0:1], in_=lamv[:, l, 0, :], axis=AX.X, op=ALU.add)
            I("dve", "tensor_reduce", (r_c2,), (r_c2,), out=lams[:, l, 1:2], in_=lamv[:, l, 2, :], axis=AX.X, op=ALU.add)
            I("act", "activation", (r_c2,), (r_c2,), out=lams[:, l, 0:2], in_=lams[:, l, 0:2], func=AF.Exp)
            I("dve", "scalar_tensor_tensor", (r_c2,), (r_c2,), out=neglam[:, l:l + 1], in0=lams[:, l, 1:2], scalar=float(-lam0),
              in1=lams[:, l, 0:1], op0=ALU.add, op1=ALU.subtract)

        rpbpad = rpbpad_t.ap()
        zt = sbuf(es, "zt", [1, 128], F32)
        I("pool", "memset", (), (r_c2,), zt[:], 0.0)
        nrpb = depth * 8 * 15 * 31
        DMA("pool", bass.AP(rpbpad_t, 0, [[0, 1], [1, 128]]), zt[:], (r_c2,), ())
        DMA("pool", bass.AP(rpbpad_t, 128 + nrpb, [[0, 1], [1, 128]]), zt[:], (r_c2,), ())
        r_dummy = Res("dummy")
        cres.append(r_dummy)
        DMA("pool", bass.AP(rpbpad_t, 128, [[0, 1], [1, nrpb]]),
            bass.AP(Wd["rpb"], 0, [[0, 1], [1, nrpb]]), (), (r_dummy,))

        for l in range(depth):
            for g in range(20):
                DMA("pool", wbf["in", l][g], W["w_in"][l][:, g * 512:(g + 1) * 512].rearrange("(kc p) n -> p kc n", p=128), (), (r_dummy,))
            for g in range(4):
                DMA("pool", wbf["bra", l][g], W["w_br_a"][l][:, g * 512:(g + 1) * 512].rearrange("(kc p) n -> p kc n", p=128), (), (r_dummy,))
                DMA("pool", wbf["brb", l][g], W["w_br_b"][l][:, g * 512:(g + 1) * 512].rearrange("(kc p) n -> p kc n", p=128), (), (r_dummy,))
                DMA("pool", wbf["out", l][g], W["w_out"][l][:, g * 512:(g + 1) * 512].rearrange("(kc p) n -> p kc n", p=128), (), (r_dummy,))
            for g in range(22):
                DMA("pool", wbf["up", l][g], W["w_ffn_up"][l][:, g * 512:(g + 1) * 512].rearrange("(kc p) n -> p kc n", p=128), (), (r_dummy,))
            for go in range(4):
                for kr in range(4):
                    DMA("pool", wbf["dn", l][go * 4 + kr],
                        W["w_ffn_down"][l][kr * 1408:(kr + 1) * 1408, go * 512:(go + 1) * 512].rearrange("(kc p) n -> p kc n", p=128),
                        (), (r_dummy,))
        k.barrier()

        class Ring:
            def __init__(self, stack, name, n, shape, dt, plist):
                self.t = [sbuf(stack, "%s%d" % (name, i), shape, dt) for i in range(n)]
                self.r = [Res("%s%d" % (name, i)) for i in range(n)]
                plist.extend(self.r)
                self.i = 0
                self.n = n

            def next(self):
                i = self.i % self.n
                self.i += 1
                return self.t[i], self.r[i]

        evac_flip = [0]

        def rms_stats(src, r_src, nch, N, sq_ring, rstd, r_rstd, bnk):
            for c in range(nch):
                sq, r_sq = sq_ring.next()
                I("act", "activation", (r_src,), (r_sq,), out=sq[:, :N], in_=src[:, c, :N], func=AF.Square)
                MM(bank[bnk][:, :N], ones[:], sq[:, :N], c == 0, c == nch - 1, (r_sq, r_c2), (rb[bnk],))
            I("act", "activation", (rb[bnk],), (r_rstd,), out=rstd[:, :N], in_=bank[bnk][:, :N], func=AF.Sqrt,
              scale=1.0 / (nch * 128), bias=EPS)
            I("dve", "reciprocal", (r_rstd,), (r_rstd,), out=rstd[:, :N], in_=rstd[:, :N])

        def phase_in(j):
            T = jobs[j]
            pl = []
            with ExitStack() as st:
                xr = Ring(st, "p0x", 2, [128, D], F32, pl)
                xo = Ring(st, "p0o", 2, [128, 16, 128], F32, pl)
                blocks = [(W["meta_tokens"], 0, NM, 0)] + [(xin[j], b * 128, 128, NM + b * 128) for b in range(T // 128)]
                for src, r0, n, pos in blocks:
                    xt, r_xt = xr.next()
                    DMA("sp", xt[:n, :], src[r0:r0 + n, :], (), (r_xt,))
                    ot, r_ot = xo.next()
                    for q4 in range(4):
                        b = q4 % 2
                        for i in range(4):
                            c = q4 * 4 + i
                            k.ins("pe", lambda e, b=b, i=i, c=c, xt=xt, n=n: e.transpose(
                                out=bank[b][:, i * 128:i * 128 + n], in_=xt[:n, c * 128:(c + 1) * 128], identity=ident[:n, :n]),
                                (r_xt, r_c), (rb[b],))
                        src_ap = bank[b][:].rearrange("p (i t) -> p i t", i=4)[:, :, :n]
                        if q4 % 2 == 0:
                            I("act", "activation", (rb[b],), (r_ot,), out=ot[:, q4 * 4:(q4 + 1) * 4, :n], in_=src_ap, func=AF.Copy)
                        else:
                            I("dve", "tensor_copy", (rb[b],), (r_ot,), out=ot[:, q4 * 4:(q4 + 1) * 4, :n], in_=src_ap)
                    DMA("pool", SC[j]["xA"][:, :, pos:pos + n].rearrange("c p t -> p c t"), ot[:, :, :n], (r_ot,), ())
                k.barrier()
            k.release(pl)

        def phase_out(j):
            T = jobs[j]
            pl = []
            with ExitStack() as st:
                xr = Ring(st, "p6x", 2, [128, 16, 128], F32, pl)
                xo = Ring(st, "p6o", 2, [128, D], F32, pl)
                for b in range(T // 128):
                    pos = NM + b * 128
                    xt, r_xt = xr.next()
                    DMA("sp", xt[:], SC[j]["xA"][:, :, pos:pos + 128].rearrange("c p t -> p c t"), (), (r_xt,))
                    ot, r_ot = xo.next()
                    for q4 in range(4):
                        bb = q4 % 2
                        for i in range(4):
                            c = q4 * 4 + i
                            k.ins("pe", lambda e, bb=bb, i=i, c=c, xt=xt: e.transpose(
                                out=bank[bb][:, i * 128:(i + 1) * 128], in_=xt[:, c, :], identity=ident[:]),
                                (r_xt, r_c), (rb[bb],))
                        if q4 % 2 == 0:
                            I("act", "activation", (rb[bb],), (r_ot,), out=ot[:, q4 * 512:(q4 + 1) * 512], in_=bank[bb][:], func=AF.Copy)
                        else:
                            I("dve", "tensor_copy", (rb[bb],), (r_ot,), out=ot[:, q4 * 512:(q4 + 1) * 512], in_=bank[bb][:])
                    DMA("pool", yout[j][b * 128:(b + 1) * 128, :], ot[:], (r_ot,), ())
                k.barrier()
            k.release(pl)

        def pos_tiles(T):
            return [(0, NM)] + [(NM + i * 512, 512) for i in range(T // 512)]

        def phase_proj(j, l):
            T = jobs[j]
            L = T + NM
            S = SC[j]
            pl = []
            with ExitStack() as st:
                xt = sbuf(st, "p1x", [128, 16, 512], F32); r_xt = Res("p1x")
                hT = sbuf(st, "p1h", [128, 16, 512], BF16); r_hT = Res("p1h")
                rstd = sbuf(st, "p1r", [128, 512], F32); r_rstd = Res("p1r")
                cs = sbuf(st, "p1cs", [128, 2, 512], F32); r_cs = Res("p1cs")
                pl.extend([r_xt, r_hT, r_rstd, r_cs])
                sqr = Ring(st, "p1sq", 2, [128, 512], F32, pl)
                slab = Ring(st, "p1w", 4, [128, 16, 512], BF16, pl)
                stg = Ring(st, "p1st", 3, [128, 4, 512], BF16, pl)
                qraw = Ring(st, "p1q", 2, [128, 512], BF16, pl)
                t1r = Ring(st, "p1t1", 2, [128, 512], F32, pl)
                t2r = Ring(st, "p1t2", 2, [128, 512], F32, pl)
                pb = [0]

                def nextbank():
                    b = 1 + (pb[0] % 5)
                    pb[0] += 1
                    return b

                for (p0, N) in pos_tiles(T):
                    DMA("sp", xt[:, :, :N], S["xA"][:, :, p0:p0 + N].rearrange("c p t -> p c t"), (), (r_xt,))
                    DMA("sp", cs[:, :, :N], cs_in[j][:, :, p0:p0 + N].rearrange("a p t -> p a t"), (), (r_cs,))
                    rms_stats(xt, r_xt, 16, N, sqr, rstd, r_rstd, 0)
                    for c in range(16):
                        I("dve", "scalar_tensor_tensor", (r_xt, r_rstd, r_c2), (r_hT,), out=hT[:, c, :N], in0=xt[:, c, :N],
                          scalar=gam[:, l, 0, c:c + 1], in1=rstd[:, :N], op0=ALU.mult, op1=ALU.mult)
                    for g in range(20):
                        wt, r_wt = slab.next()
                        DMA("sp", wt[:], wbf["in", l][g], (), (r_wt,))
                        if g in (4, 5, 10, 11):
                            dst = S["va"] if g < 6 else S["vb"]
                            col0 = (g - 4) * 512 if g < 6 else (g - 10) * 512
                            for s0 in range(0, N, 128):
                                n = min(128, N - s0)
                                b = nextbank()
                                for kc in range(16):
                                    MM(bank[b][:n, :], hT[:, kc, s0:s0 + n], wt[:, kc, :], kc == 0, kc == 15, (r_hT, r_wt), (rb[b],))
                                so, r_so = qraw.next()
                                I("act", "activation", (rb[b],), (r_so,), out=so[:n, :], in_=bank[b][:n, :], func=AF.Copy)
                                DMA("pool", dst[p0 + s0:p0 + s0 + n, col0:col0 + 512], so[:n, :], (r_so,), ())
                            continue
                        so, r_so = stg.next()
                        for oc in range(4):
                            b = nextbank()
                            for kc in range(16):
                                MM(bank[b][:, :N], wt[:, kc, oc * 128:(oc + 1) * 128], hT[:, kc, :N], kc == 0, kc == 15, (r_hT, r_wt), (rb[b],))
                            if g < 4:
                                qr, r_qr = qraw.next()
                                I("act", "activation", (rb[b],), (r_qr,), out=qr[:, :N], in_=bank[b][:, :N], func=AF.Copy)
                                MM(bank[7][:, :N], rotb[:], qr[:, :N], True, True, (r_qr, r_c2), (rb[7],))
                                t1, r_t1 = t1r.next()
                                t2, r_t2 = t2r.next()
                                I("dve", "tensor_tensor", (rb[b], r_cs), (r_t1,), out=t1[:, :N], in0=bank[b][:, :N], in1=cs[:, 0, :N], op=ALU.mult)
                                I("dve", "tensor_tensor", (rb[7], r_cs), (r_t2,), out=t2[:, :N], in0=bank[7][:, :N], in1=cs[:, 1, :N], op=ALU.mult)
                                I("pool", "tensor_tensor", (r_t1, r_t2), (r_so,), out=so[:, oc, :N], in0=t1[:, :N], in1=t2[:, :N], op=ALU.add)
                            elif g in (6, 7):
                                I("act", "activation", (rb[b],), (r_so,), out=so[:, oc, :N], in_=bank[b][:, :N], func=AF.Copy, scale=SCALE)
                            elif g in (8, 9):
                                I("act", "activation", (rb[b],), (r_so,), out=so[:, oc, :N], in_=bank[b][:, :N], func=AF.Copy)
                            else:
                                I("act", "activation", (rb[b],), (r_so,), out=so[:, oc, :N], in_=bank[b][:, :N], func=AF.Sigmoid)
                        if g < 2:
                            dst = S["qaT"][g * 4:(g + 1) * 4]
                        elif g < 4:
                            dst = S["kaT"][(g - 2) * 4:(g - 1) * 4]
                        elif g < 8:
                            dst = S["qbT"][(g - 6) * 4:(g - 5) * 4]
                        elif g < 10:
                            dst = S["kbT"][(g - 8) * 4:(g - 7) * 4]
                        elif g < 16:
                            dst = S["gaT"][(g - 12) * 4:(g - 11) * 4]
                        else:
                            dst = S["gbT"][(g - 16) * 4:(g - 15) * 4]
                        DMA("pool", dst[:, :, p0:p0 + N].rearrange("c p t -> p c t"), so[:, :, :N], (r_so,), ())
                k.barrier()
            k.release(pl)

        def phase_diff(j, l):
            T = jobs[j]
            L = T + NM
            S = SC[j]
            nkb = (L + 127) // 128
            kbs = [(i * 128, min(128, L - i * 128)) for i in range(nkb)]
            pl = []
            with ExitStack() as st:
                vp = sbuf(st, "p2v", [128, nkb, 257], BF16); r_vp = Res("p2v")
                kT = sbuf(st, "p2k", [128, 2, L], BF16); r_kT = Res("p2k")
                qT = sbuf(st, "p2q", [128, 2, L], BF16); r_qT = Res("p2q")
                stt = sbuf(st, "p2s", [128, 2, 2, 40], F32); r_stt = Res("p2s")
                negc = sbuf(st, "p2c", [128, 2], F32); r_negc = Res("p2c")
                pl.extend([r_vp, r_kT, r_qT, r_stt, r_negc])
                sqr = Ring(st, "p2sq", 2, [128, 512], F32, pl)
                ptr = Ring(st, "p2p", 6, [128, 512], BF16, pl)
                accs = Ring(st, "p2a", 4, [128, 4, 257], F32, pl)
                otr = Ring(st, "p2o", 2, [128, 256], F32, pl)
                o2r = Ring(st, "p2o2", 2, [128, 256], F32, pl)
                obr = Ring(st, "p2ob", 2, [128, 4, 256], BF16, pl)
                oTr = Ring(st, "p2oT", 2, [128, 2, 512], BF16, pl)
                smr = Ring(st, "p2sm", 4, [128, 8], F32, pl)
                I("pool", "memset", (), (r_vp,), vp[:, :, 256:257], 1.0)
                sb_i = [0]
                for h in range(4):
                    for c in range(2):
                        DMA("sp", kT[:, c, :], S["kaT"][h * 2 + c], (), (r_kT,))
                        DMA("sp", qT[:, c, :], S["qaT"][h * 2 + c], (), (r_qT,))
                    for (k0, kn) in kbs:
                        DMA("sp", vp[:kn, k0 // 128, 0:256], S["va"][k0:k0 + kn, h * 256:(h + 1) * 256], (), (r_vp,))
                    chunks = [(i * 512, min(512, L - i * 512)) for i in range((L + 511) // 512)]
                    for c in range(2):
                        for wi, (src, r_src) in enumerate(((kT, r_kT), (qT, r_qT))):
                            for ci, (c0, cn) in enumerate(chunks):
                                sq, r_sq = sqr.next()
                                I("act", "activation", (r_src,), (r_sq,), out=sq[:, :cn], in_=src[:, c, c0:c0 + cn], func=AF.Square)
                                MM(bank[0][:, :cn], ones[:], sq[:, :cn], True, True, (r_sq, r_c2), (rb[0],))
                                I("dve", "tensor_reduce", (rb[0],), (r_stt,), out=stt[:, c, wi, ci:ci + 1], in_=bank[0][:, :cn], axis=AX.X, op=ALU.max)
                            I("dve", "tensor_reduce", (r_stt,), (r_stt,), out=stt[:, c, wi, 39:40], in_=stt[:, c, wi, 0:len(chunks)], axis=AX.X, op=ALU.max)
                        I("dve", "tensor_tensor", (r_stt,), (r_stt,), out=stt[:, c, 0, 38:39], in0=stt[:, c, 0, 39:40], in1=stt[:, c, 1, 39:40], op=ALU.mult)
                        I("act", "activation", (r_stt,), (r_stt,), out=stt[:, c, 0, 38:39], in_=stt[:, c, 0, 38:39], func=AF.Sqrt)
                        I("dve", "tensor_scalar", (r_stt,), (r_negc,), out=negc[:, c:c + 1], in0=stt[:, c, 0, 38:39], scalar1=float(-SCALE), scalar2=None, op0=ALU.mult)
                    def epilogue(q0, N, acc):
                        nst = (N + 127) // 128
                        (a1, r_a1), (a2, r_a2) = acc
                        ob, r_ob = obr.next()
                        for s_ in range(nst):
                            n = min(128, N - s_ * 128)
                            sm, r_sm = smr.next()
                            I("dve", "reciprocal", (r_a1,), (r_sm,), out=sm[:n, 0:1], in_=a1[:n, s_, 256:257])
                            I("dve", "reciprocal", (r_a2,), (r_sm,), out=sm[:n, 1:2], in_=a2[:n, s_, 256:257])
                            I("dve", "tensor_tensor", (r_sm, r_c2), (r_sm,), out=sm[:n, 2:3], in0=sm[:n, 1:2], in1=neglam[:n, l:l + 1], op=ALU.mult)
                            ot, r_ot = otr.next()
                            o2, r_o2 = o2r.next()
                            I("dve", "tensor_scalar", (r_a2, r_sm), (r_ot,), out=ot[:n, :], in0=a2[:n, s_, 0:256], scalar1=sm[:n, 2:3], scalar2=None, op0=ALU.mult)
                            I("dve", "scalar_tensor_tensor", (r_a1, r_sm, r_ot), (r_o2,), out=o2[:n, :], in0=a1[:n, s_, 0:256], scalar=sm[:n, 0:1],
                              in1=ot[:n, :], op0=ALU.mult, op1=ALU.add)
                            I("act", "activation", (r_o2,), (r_ot, r_sm), out=ot[:n, :], in_=o2[:n, :], func=AF.Square, accum_out=sm[:n, 3:4])
                            I("act", "activation", (r_sm,), (r_sm,), out=sm[:n, 4:5], in_=sm[:n, 3:4], func=AF.Sqrt, scale=1.0 / 256, bias=EPS)
                            I("dve", "reciprocal", (r_sm,), (r_sm,), out=sm[:n, 5:6], in_=sm[:n, 4:5])
                            I("dve", "scalar_tensor_tensor", (r_o2, r_sm, r_c2), (r_ob,), out=ob[:n, s_, :], in0=o2[:n, :], scalar=sm[:n, 5:6],
                              in1=subl[:n, l, :], op0=ALU.mult, op1=ALU.mult)
                        oT, r_oT = oTr.next()
                        pbv = bank[7][:].bitcast(BF16)
                        for f in range(2):
                            for s_ in range(nst):
                                n = min(128, N - s_ * 128)
                                k.ins("pe", lambda e, f=f, s_=s_, n=n, ob=ob, pbv=pbv: e.transpose(
                                    out=pbv[:, f * 512 + s_ * 128:f * 512 + s_ * 128 + n], in_=ob[:n, s_, f * 128:(f + 1) * 128], identity=identb[:n, :n]),
                                    (r_ob, r_c), (rb[7],))
                        I("dve", "tensor_copy", (rb[7],), (r_oT,), out=oT[:, :, :N], in_=pbv.rearrange("p (f t) -> p f t", f=2)[:, :, :N])
                        DMA("pool", S["oaT"][h * 2:h * 2 + 2, :, q0:q0 + N].rearrange("c p t -> p c t"), oT[:, :, :N], (r_oT,), ())

                    accst = {}

                    def emit_A(step):
                        q0, N, c, k0, kn = step
                        sbk = 4 + (sb_i[0] % 4)
                        sb_i[0] += 1
                        MM(bank[sbk][:kn, :N], kT[:, c, k0:k0 + kn], qT[:, c, q0:q0 + N], True, True, (r_kT, r_qT), (rb[sbk],))
                        pt, r_pt = ptr.next()
                        I("act", "activation", (rb[sbk], r_negc), (r_pt,), out=pt[:kn, :N], in_=bank[sbk][:kn, :N], func=AF.Exp,
                          scale=SCALE, bias=negc[:kn, c:c + 1])
                        return (step, pt, r_pt)

                    def emit_B(item):
                        (q0, N, c, k0, kn), pt, r_pt = item
                        nst = (N + 127) // 128
                        for s_ in range(nst):
                            n = min(128, N - s_ * 128)
                            MM(bank[s_][:n, 0:257], pt[:kn, s_ * 128:s_ * 128 + n], vp[:kn, k0 // 128, :], k0 == 0, k0 == kbs[-1][0],
                               (r_pt, r_vp), (rb[s_],))
                        if k0 == kbs[-1][0]:
                            at, r_at = accs.next()
                            for s_ in range(nst):
                                n = min(128, N - s_ * 128)
                                if (s_ + c) % 2 == 0:
                                    I("act", "activation", (rb[s_],), (r_at,), out=at[:n, s_, :], in_=bank[s_][:n, 0:257], func=AF.Copy)
                                else:
                                    I("dve", "tensor_copy", (rb[s_],), (r_at,), out=at[:n, s_, :], in_=bank[s_][:n, 0:257])
                            accst.setdefault(q0, []).append((at, r_at))
                            if c == 1:
                                epilogue(q0, N, accst.pop(q0))

                    steps = [(q0, N, c, k0, kn) for (q0, N) in pos_tiles(T) for c in range(2) for (k0, kn) in kbs]
                    pend = []
                    for step in steps:
                        pend.append(emit_A(step))
                        if len(pend) > 3:
                            emit_B(pend.pop(0))
                    while pend:
                        emit_B(pend.pop(0))
                k.barrier()
            k.release(pl)

        def phase_na(j, l):
            T = jobs[j]
            L = T + NM
            S = SC[j]
            rows = T // 64
            pl = []
            with ExitStack() as st:
                hk = sbuf(st, "p3hk", [64, 8, 15, 64], BF16); r_hk = Res("p3hk")
                hk32 = sbuf(st, "p3hk32", [64, 8, 15, 64], F32); r_hk32 = Res("p3hk32")
                km = sbuf(st, "p3km", [128, 8, NM], BF16); r_km = Res("p3km")
                qm = sbuf(st, "p3qm", [128, 8, NM], BF16); r_qm = Res("p3qm")
                vm = sbuf(st, "p3vm", [NM, 8, 129], BF16); r_vm = Res("p3vm")
                pl.extend([r_hk, r_hk32, r_km, r_qm, r_vm])
                qtr = Ring(st, "p3q", 2, [128, 8, 512], BF16, pl)
                kwr = Ring(st, "p3k", 2, [128, 8, 1024], BF16, pl)
                vwr = Ring(st, "p3v", 2, [64, 16, 8, 129], BF16, pl)
                sqr = Ring(st, "p3sq", 2, [128, 512], F32, pl)
                sttr = Ring(st, "p3st", 2, [128, 64], F32, pl)
                ptr = Ring(st, "p3p", 3, [64, 512], BF16, pl)
                pmr = Ring(st, "p3pm", 3, [NM, 64], BF16, pl)
                obr = Ring(st, "p3ob", 3, [64, 8, 128], BF16, pl)
                oTr = Ring(st, "p3oT", 2, [128, 8, 512], BF16, pl)
                smr = Ring(st, "p3sm", 4, [64, 8], F32, pl)
                base = 128 + l * 8 * 15 * 31 - 48
                DMA("sp", hk32[:], bass.AP(rpbpad_t, base, [[1, 64], [465, 8], [31, 15], [1, 64]]), (), (r_hk32,))
                I("dve", "tensor_copy", (r_hk32,), (r_hk,), out=hk[:], in_=hk32[:])
                for v_ in vwr.t:
                    I("pool", "memset", (), (vwr.r[vwr.t.index(v_)],), v_[:, :, :, 128:129], 1.0)
                I("pool", "memset", (), (r_vm,), vm[:, :, 128:129], 1.0)
                DMA("sp", km[:], S["kbT"][:, :, 0:NM].rearrange("h p t -> p h t"), (), (r_km,))
                DMA("sp", qm[:], S["qbT"][:, :, 0:NM].rearrange("h p t -> p h t"), (), (r_qm,))
                DMA("sp", vm[:, :, 0:128], S["vb"][0:NM, :].rearrange("t (h d) -> t h d", h=8), (), (r_vm,))
                sb_i = [0]

                def shift_bound(parts, r_parts, stt, r_stt):
                    for wi in range(2):
                        cnt = 0
                        for (src, r_src, w) in parts[wi]:
                            for h in range(8):
                                for c0 in range(0, w, 512):
                                    cn = min(512, w - c0)
                                    sq, r_sq = sqr.next()
                                    I("act", "activation", (r_src,), (r_sq,), out=sq[:, :cn], in_=src[:, h, c0:c0 + cn], func=AF.Square)
                                    MM(bank[0][:, :cn], ones[:], sq[:, :cn], True, True, (r_sq, r_c2), (rb[0],))
                                    I("dve", "tensor_reduce", (rb[0],), (r_stt,), out=stt[:, wi * 28 + cnt:wi * 28 + cnt + 1], in_=bank[0][:, :cn], axis=AX.X, op=ALU.max)
                                    cnt += 1
                        I("dve", "tensor_reduce", (r_stt,), (r_stt,), out=stt[:, 56 + wi:57 + wi], in_=stt[:, wi * 28:wi * 28 + cnt], axis=AX.X, op=ALU.max)
                    I("dve", "tensor_tensor", (r_stt,), (r_stt,), out=stt[:, 58:59], in0=stt[:, 56:57], in1=stt[:, 57:58], op=ALU.mult)
                    I("act", "activation", (r_stt,), (r_stt,), out=stt[:, 59:60], in_=stt[:, 58:59], func=AF.Sqrt)
                    I("dve", "tensor_scalar", (r_stt,), (r_stt,), out=stt[:, 60:61], in0=stt[:, 59:60], scalar1=-1.0, scalar2=-1.0, op0=ALU.mult, op1=ALU.add)

                def att_A(h, qsrc, qc0, nq, win, stt, r_q, r_kw, r_vw, r_stt, ob, r_ob, kw=None, vw=None):
                    bS = 1 + (sb_i[0] % 2)
                    bM = 3 + (sb_i[0] % 2)
                    bO = 5 + (sb_i[0] % 2)
                    sb_i[0] += 1
                    nw = 0
                    pt = r_pt = None
                    w0 = 0
                    if win is not None:
                        w0, dr0 = win
                        nw = 8
                        MM(bank[bS][0:64, 0:512], identb[0:64, 0:64], maskb[:, :], True, False, (r_c, r_c2), (rb[bS],))
                        for i in range(8):
                            MM(bank[bS][0:64, i * 64:(i + 1) * 64], kw[:, h, (w0 + i) * 64:(w0 + i + 1) * 64], qsrc[:, h, qc0:qc0 + nq],
                               False, False, (r_kw, r_q), (rb[bS],))
                        for i in range(8):
                            MM(bank[bS][0:64, i * 64:(i + 1) * 64], hk[:, h, dr0 + i, :], j64b[:, :], False, i == 7, (r_hk, r_c2), (rb[bS],))
                    MM(bank[bM][0:NM, 0:nq], km[:, h, :], qsrc[:, h, qc0:qc0 + nq], True, True, (r_km, r_q), (rb[bM],))
                    pm, r_pm = pmr.next()
                    I("act", "activation", (rb[bM], r_stt), (r_pm,), out=pm[:, :nq], in_=bank[bM][0:NM, 0:nq], func=AF.Exp, bias=stt[0:NM, 60:61])
                    if nw:
                        pt, r_pt = ptr.next()
                        I("act", "activation", (rb[bS], r_stt), (r_pt,), out=pt[:, :], in_=bank[bS][0:64, 0:512], func=AF.Exp, bias=stt[0:64, 60:61])
                    return (h, nq, nw, w0, bO, pm, r_pm, pt, r_pt, vw, r_vw, ob, r_ob)

                def att_B(rec):
                    h, nq, nw, w0, bO, pm, r_pm, pt, r_pt, vw, r_vw, ob, r_ob = rec
                    if nw:
                        for i in range(8):
                            MM(bank[bO][0:nq, 0:129], pt[:, i * 64:(i + 1) * 64], vw[:, w0 + i, h, :], i == 0, False, (r_pt, r_vw), (rb[bO],))
                    MM(bank[bO][0:nq, 0:129], pm[:, :nq], vm[:, h, :], nw == 0, True, (r_pm, r_vm), (rb[bO],))
                    sm, r_sm = smr.next()
                    I("dve", "reciprocal", (rb[bO],), (r_sm,), out=sm[:nq, 0:1], in_=bank[bO][0:nq, 128:129])
                    I("dve", "tensor_scalar", (rb[bO], r_sm), (r_ob,), out=ob[:nq, h, :], in0=bank[bO][0:nq, 0:128], scalar1=sm[:nq, 0:1], scalar2=None, op0=ALU.mult)

                def attend(qsrc, qc0, nq, win, stt, r_q, r_kw, r_vw, r_stt, ob, r_ob, kw=None, vw=None):
                    for h in range(8):
                        att_B(att_A(h, qsrc, qc0, nq, win, stt, r_q, r_kw, r_vw, r_stt, ob, r_ob, kw=kw, vw=vw))

                def flush(ob, r_ob, nq, oT, r_oT, col0):
                    pbv = bank[7][:].bitcast(BF16)
                    for h in range(8):
                        k.ins("pe", lambda e, h=h, ob=ob, nq=nq, pbv=pbv: e.transpose(
                            out=pbv[:, h * 64:h * 64 + nq], in_=ob[:nq, h, :], identity=identb[:nq, :nq]), (r_ob, r_c), (rb[7],))
                    I("dve", "tensor_copy", (rb[7],), (r_oT,), out=oT[:, :, col0:col0 + nq], in_=pbv[:, 0:512].rearrange("p (h t) -> p h t", h=8)[:, :, :nq])

                stt, r_stt = sttr.next()
                shift_bound([[(qm, r_qm, NM)], [(km, r_km, NM)]], None, stt, r_stt)
                ob, r_ob = obr.next()
                oT, r_oT = oTr.next()
                attend(qm, 0, NM, None, stt, r_qm, None, None, r_stt, ob, r_ob)
                flush(ob, r_ob, NM, oT, r_oT, 0)
                DMA("pool", S["obT"][:, :, 0:NM].rearrange("h p t -> p h t"), oT[:, :, 0:NM], (r_oT,), ())
                for ti in range(T // 512):
                    r0 = ti * 8
                    wr0 = max(0, min(r0 - 4, rows - 16))
                    wr0 = min(wr0, max(0, rows - 16))
                    nwr = min(16, rows - wr0)
                    qt, r_qt = qtr.next()
                    kw, r_kw = kwr.next()
                    vw, r_vw = vwr.next()
                    p0 = NM + ti * 512
                    DMA("sp", qt[:], S["qbT"][:, :, p0:p0 + 512].rearrange("h p t -> p h t"), (), (r_qt,))
                    DMA("sp", kw[:, :, 0:nwr * 64], S["kbT"][:, :, NM + wr0 * 64:NM + (wr0 + nwr) * 64].rearrange("h p t -> p h t"), (), (r_kw,))
                    for rr in range(nwr):
                        DMA("sp", vw[:, rr, :, 0:128], S["vb"][NM + (wr0 + rr) * 64:NM + (wr0 + rr + 1) * 64, :].rearrange("t (h d) -> t h d", h=8), (), (r_vw,))
                    stt, r_stt = sttr.next()
                    shift_bound([[(qt, r_qt, 512)], [(kw, r_kw, nwr * 64), (km, r_km, NM)]], None, stt, r_stt)
                    oT, r_oT = oTr.next()
                    pend = []

                    def do_B(item):
                        rec, jr_ = item
                        att_B(rec)
                        if rec[0] == 7:
                            flush(rec[11], rec[12], 64, oT, r_oT, jr_ * 64)

                    for jr in range(8):
                        r = r0 + jr
                        rs = max(0, min(r - 4, rows - 8))
                        dr0 = rs - r + 7
                        ob, r_ob = obr.next()
                        for h in range(8):
                            pend.append((att_A(h, qt, jr * 64, 64, (rs - wr0, dr0), stt, r_qt, r_kw, r_vw, r_stt, ob, r_ob, kw=kw, vw=vw), jr))
                            if len(pend) > 1:
                                do_B(pend.pop(0))
                    while pend:
                        do_B(pend.pop(0))
                    DMA("pool", S["obT"][:, :, p0:p0 + 512].rearrange("h p t -> p h t"), oT[:], (r_oT,), ())
                k.barrier()
            k.release(pl)

        def post_norm_residual(yT, r_yT, xt, r_xt, N, l, gi, rstd, r_rstd, tmpr, c_lo=0, x_off=0):
            for c in range(16):
                tm, r_tm = tmpr.next()
                I("dve", "scalar_tensor_tensor", (r_yT, r_rstd, r_c2), (r_tm,), out=tm[:, :N], in0=yT[:, c, :N], scalar=gam[:, l, gi, c:c + 1],
                  in1=rstd[:, :N], op0=ALU.mult, op1=ALU.mult)
                I("pool", "tensor_tensor", (r_tm, r_xt), (r_yT,), out=yT[:, c, :N], in0=tm[:, :N], in1=xt[:, c, x_off:x_off + N], op=ALU.add)

        def phase_mix(j, l):
            T = jobs[j]
            L = T + NM
            S = SC[j]
            pl = []
            with ExitStack() as st:
                xt = sbuf(st, "p4x", [128, 16, 512], F32); r_xt = Res("p4x")
                oa = sbuf(st, "p4oa", [128, 8, 512], BF16); r_oa = Res("p4oa")
                obt = sbuf(st, "p4ob", [128, 8, 512], BF16); r_obt = Res("p4ob")
                mix = sbuf(st, "p4m", [128, 16, 512], BF16); r_mix = Res("p4m")
                yT = sbuf(st, "p4y", [128, 16, 512], F32); r_yT = Res("p4y")
                rstd = sbuf(st, "p4r", [128, 512], F32); r_rstd = Res("p4r")
                pl.extend([r_xt, r_oa, r_obt, r_mix, r_yT, r_rstd])
                gar = Ring(st, "p4ga", 2, [128, 4, 512], BF16, pl)
                gbr = Ring(st, "p4gb", 2, [128, 4, 512], BF16, pl)
                slab = Ring(st, "p4w", 3, [128, 16, 512], BF16, pl)
                sqr = Ring(st, "p4sq", 2, [128, 512], F32, pl)
                t1r = Ring(st, "p4t1", 2, [128, 512], F32, pl)
                t2r = Ring(st, "p4t2", 2, [128, 512], F32, pl)
                pb = [0]
                for (p0, N) in pos_tiles(T):
                    DMA("sp", oa[:, :, :N], S["oaT"][:, :, p0:p0 + N].rearrange("c p t -> p c t"), (), (r_oa,))
                    DMA("sp", obt[:, :, :N], S["obT"][:, :, p0:p0 + N].rearrange("c p t -> p c t"), (), (r_obt,))
                    DMA("sp", xt[:, :, :N], S["xA"][:, :, p0:p0 + N].rearrange("c p t -> p c t"), (), (r_xt,))
                    for go in range(4):
                        wa, r_wa = slab.next()
                        DMA("sp", wa[:, 0:8, :], wbf["bra", l][go], (), (r_wa,))
                        DMA("sp", wa[:, 8:16, :], wbf["brb", l][go], (), (r_wa,))
                        ga, r_ga = gar.next()
                        gb, r_gb = gbr.next()
                        DMA("sp", ga[:, :, :N], S["gaT"][go * 4:(go + 1) * 4, :, p0:p0 + N].rearrange("c p t -> p c t"), (), (r_ga,))
                        DMA("sp", gb[:, :, :N], S["gbT"][go * 4:(go + 1) * 4, :, p0:p0 + N].rearrange("c p t -> p c t"), (), (r_gb,))
                        for oc in range(4):
                            ba = 1 + (pb[0] % 2)
                            bb = 3 + (pb[0] % 2)
                            pb[0] += 1
                            for kc in range(8):
                                MM(bank[ba][:, :N], wa[:, kc, oc * 128:(oc + 1) * 128], oa[:, kc, :N], kc == 0, kc == 7, (r_wa, r_oa), (rb[ba],))
                            for kc in range(8):
                                MM(bank[bb][:, :N], wa[:, 8 + kc, oc * 128:(oc + 1) * 128], obt[:, kc, :N], kc == 0, kc == 7, (r_wa, r_obt), (rb[bb],))
                            t1, r_t1 = t1r.next()
                            t2, r_t2 = t2r.next()
                            I("dve", "tensor_tensor", (rb[ba], r_ga), (r_t1,), out=t1[:, :N], in0=bank[ba][:, :N], in1=ga[:, oc, :N], op=ALU.mult)
                            I("dve", "tensor_tensor", (rb[bb], r_gb), (r_t2,), out=t2[:, :N], in0=bank[bb][:, :N], in1=gb[:, oc, :N], op=ALU.mult)
                            I("pool", "tensor_tensor", (r_t1, r_t2), (r_mix,), out=mix[:, go * 4 + oc, :N], in0=t1[:, :N], in1=t2[:, :N], op=ALU.add)
                    for go in range(4):
                        wo, r_wo = slab.next()
                        DMA("sp", wo[:], wbf["out", l][go], (), (r_wo,))
                        for oc in range(4):
                            c = go * 4 + oc
                            b = 5 + (pb[0] % 2)
                            pb[0] += 1
                            for kc in range(16):
                                MM(bank[b][:, :N], wo[:, kc, oc * 128:(oc + 1) * 128], mix[:, kc, :N], kc == 0, kc == 15, (r_wo, r_mix), (rb[b],))
                            if c > 0:
                                psq, r_psq, pc = pend_sq
                                MM(bank[0][:, :N], ones[:], psq[:, :N], pc == 0, False, (r_psq, r_c2), (rb[0],))
                            I("dve", "tensor_copy", (rb[b],), (r_yT,), out=yT[:, c, :N], in_=bank[b][:, :N])
                            sq, r_sq = sqr.next()
                            I("act", "activation", (rb[b],), (r_sq,), out=sq[:, :N], in_=bank[b][:, :N], func=AF.Square)
                            pend_sq = (sq, r_sq, c)
                            if c == 15:
                                MM(bank[0][:, :N], ones[:], sq[:, :N], False, True, (r_sq, r_c2), (rb[0],))
                    I("act", "activation", (rb[0],), (r_rstd,), out=rstd[:, :N], in_=bank[0][:, :N], func=AF.Sqrt, scale=1.0 / D, bias=EPS)
                    I("dve", "reciprocal", (r_rstd,), (r_rstd,), out=rstd[:, :N], in_=rstd[:, :N])
                    post_norm_residual(yT, r_yT, xt, r_xt, N, l, 1, rstd, r_rstd, t1r)
                    DMA("pool", S["xB"][:, :, p0:p0 + N].rearrange("c p t -> p c t"), yT[:, :, :N], (r_yT,), ())
                k.barrier()
            k.release(pl)

        def phase_ffn(j, l):
            T = jobs[j]
            L = T + NM
            S = SC[j]
            pl = []
            with ExitStack() as st:
                xt = sbuf(st, "p5x", [128, 16, 512], F32); r_xt = Res("p5x")
                hT = sbuf(st, "p5h", [128, 16, 512], BF16); r_hT = Res("p5h")
                act = sbuf(st, "p5a", [128, 44, 512], BF16); r_act = Res("p5a")
                fT = sbuf(st, "p5f", [128, 16, 512], F32); r_fT = Res("p5f")
                rstd = sbuf(st, "p5r", [128, 512], F32); r_rstd = Res("p5r")
                pl.extend([r_xt, r_hT, r_act, r_fT, r_rstd])
                slab = Ring(st, "p5w", 3, [128, 16, 512], BF16, pl)
                sqr = Ring(st, "p5sq", 2, [128, 512], F32, pl)
                a1r = Ring(st, "p5a1", 2, [128, 512], F32, pl)
                a2r = Ring(st, "p5a2", 2, [128, 512], F32, pl)
                pb = [0]
                s = 0
                while s < L:
                    nout = min(510, L - s)
                    Nc = nout + 2
                    lo = s - 1
                    c_lo = 1 if lo < 0 else 0
                    c_hi = Nc - 1 if lo + Nc > L else Nc
                    if c_lo > 0 or c_hi < Nc:
                        I("pool", "memset", (), (r_xt,), xt[:, :, :Nc], 0.0)
                    DMA("sp", xt[:, :, c_lo:c_hi], S["xB"][:, :, lo + c_lo:lo + c_hi].rearrange("c p t -> p c t"), (), (r_xt,))
                    rms_stats(xt, r_xt, 16, Nc, sqr, rstd, r_rstd, 0)
                    for c in range(16):
                        I("dve", "scalar_tensor_tensor", (r_xt, r_rstd, r_c2), (r_hT,), out=hT[:, c, :Nc], in0=xt[:, c, :Nc],
                          scalar=gam[:, l, 2, c:c + 1], in1=rstd[:, :Nc], op0=ALU.mult, op1=ALU.mult)
                    for g in range(11):
                        wg, r_wg = slab.next()
                        DMA("sp", wg[:], wbf["up", l][g], (), (r_wg,))
                        wv, r_wv = slab.next()
                        DMA("sp", wv[:], wbf["up", l][11 + g], (), (r_wv,))
                        for oc in range(4):
                            cc = g * 4 + oc
                            bg = 1 + (pb[0] % 2)
                            bv = 3 + (pb[0] % 2)
                            pb[0] += 1
                            for kc in range(16):
                                MM(bank[bg][:, :Nc], wg[:, kc, oc * 128:(oc + 1) * 128], hT[:, kc, :Nc], kc == 0, kc == 15, (r_wg, r_hT), (rb[bg],))
                            for kc in range(16):
                                MM(bank[bv][:, :Nc], wv[:, kc, oc * 128:(oc + 1) * 128], hT[:, kc, :Nc], kc == 0, kc == 15, (r_wv, r_hT), (rb[bv],))
                            a1, r_a1 = a1r.next()
                            a2, r_a2 = a2r.next()
                            I("dve", "tensor_scalar", (rb[bg], r_c2), (r_a1,), out=a1[:, :nout], in0=bank[bg][:, 1:1 + nout], scalar1=cw[:, l, 1, cc:cc + 1],
                              scalar2=cw[:, l, 3, cc:cc + 1], op0=ALU.mult, op1=ALU.add)
                            I("dve", "scalar_tensor_tensor", (rb[bg], r_a1, r_c2), (r_a2,), out=a2[:, :nout], in0=bank[bg][:, 0:nout], scalar=cw[:, l, 0, cc:cc + 1],
                              in1=a1[:, :nout], op0=ALU.mult, op1=ALU.add)
                            I("dve", "scalar_tensor_tensor", (rb[bg], r_a2, r_c2), (r_a1,), out=a1[:, :nout], in0=bank[bg][:, 2:2 + nout], scalar=cw[:, l, 2, cc:cc + 1],
                              in1=a2[:, :nout], op0=ALU.mult, op1=ALU.add)
                            I("act", "activation", (r_a1,), (r_a2,), out=a2[:, :nout], in_=a1[:, :nout], func=AF.Gelu_apprx_tanh)
                            I("dve", "tensor_tensor", (rb[bv], r_a2), (r_act,), out=act[:, cc, :nout], in0=bank[bv][:, 1:1 + nout], in1=a2[:, :nout], op=ALU.mult)
                    for go in range(4):
                        for kr in range(4):
                            wd, r_wd = slab.next()
                            DMA("sp", wd[:, 0:11, :], wbf["dn", l][go * 4 + kr], (), (r_wd,))
                            for oc in range(4):
                                for kc in range(11):
                                    MM(bank[4 + oc][:, :nout], wd[:, kc, oc * 128:(oc + 1) * 128], act[:, kr * 11 + kc, :nout],
                                       kr == 0 and kc == 0, kr == 3 and kc == 10, (r_wd, r_act), (rb[4 + oc],))
                        for oc in range(4):
                            c = go * 4 + oc
                            b = 4 + oc
                            I("dve", "tensor_copy", (rb[b],), (r_fT,), out=fT[:, c, :nout], in_=bank[b][:, :nout])
                            sq, r_sq = sqr.next()
                            I("act", "activation", (rb[b],), (r_sq,), out=sq[:, :nout], in_=bank[b][:, :nout], func=AF.Square)
                            MM(bank[0][:, :nout], ones[:], sq[:, :nout], c == 0, c == 15, (r_sq, r_c2), (rb[0],))
                    I("act", "activation", (rb[0],), (r_rstd,), out=rstd[:, :nout], in_=bank[0][:, :nout], func=AF.Sqrt, scale=1.0 / D, bias=EPS)
                    I("dve", "reciprocal", (r_rstd,), (r_rstd,), out=rstd[:, :nout], in_=rstd[:, :nout])
                    post_norm_residual(fT, r_fT, xt, r_xt, nout, l, 3, rstd, r_rstd, a1r, x_off=1)
                    DMA("pool", S["xA"][:, :, s:s + nout].rearrange("c p t -> p c t"), fT[:, :, :nout], (r_fT,), ())
                    s += nout
                k.barrier()
            k.release(pl)

        def on(name, l=0):
            return phases is None or name in phases or (name, l) in phases
        for j in range(len(jobs)):
            if on("in"):
                phase_in(j)
            for l in range(depth):
                if on("proj", l):
                    phase_proj(j, l)
                if on("diff", l):
                    phase_diff(j, l)
                if on("na", l):
                    phase_na(j, l)
                if on("mix", l):
                    phase_mix(j, l)
                if on("ffn", l):
                    phase_ffn(j, l)
            if on("out"):
                phase_out(j)
        k.barrier()
        k.emit()
    return nc


def host_consts(jobs):
    c = {}
    rot = np.zeros((128, 128), np.float32)
    for m in range(64):
        rot[m + 64, m] = -1.0
    for m in range(64, 128):
        rot[m - 64, m] = 1.0
    c["c_rot"] = rot
    c["c_j64"] = np.ascontiguousarray(np.eye(64, dtype=np.float32)[::-1])
    cstart = np.clip(np.arange(64) - 8, 0, 48)
    cp = np.arange(64)[:, None]
    cq = np.arange(64)[None, :]
    inwin = (cp >= cstart[None, :]) & (cp < cstart[None, :] + 16)
    m = np.where(inwin, 0.0, NEG).astype(np.float32)
    c["c_mask"] = np.ascontiguousarray(np.tile(m, (1, 8)))
    for j, T in enumerate(jobs):
        L = T + NM
        inv = (1.0 / (np.float32(10000.0) ** (np.arange(0, 128, 2, dtype=np.float32) / np.float32(128)))).astype(np.float32)
        ang = (np.arange(L, dtype=np.float32)[:, None] * inv[None, :]).astype(np.float32)
        cos = np.cos(ang).astype(np.float32).T
        sin = np.sin(ang).astype(np.float32).T
        c["cs%d" % j] = np.ascontiguousarray(np.stack([np.concatenate([cos, cos], 0), np.concatenate([sin, sin], 0)], 0))
    return c


_CACHE = {}


def kernel(**inputs):
    x_prompt = np.asarray(inputs["x_prompt"], np.float32)
    x_sample = np.asarray(inputs["x_sample"], np.float32)
    jobs = (x_sample.shape[1], x_prompt.shape[1])
    key = jobs
    if key not in _CACHE:
        _CACHE[key] = build_program(list(jobs), depth=2)
    nc = _CACHE[key]
    consts = host_consts(jobs)
    wts = {n: np.ascontiguousarray(np.asarray(inputs[n], np.float32)) for n, _ in WSPEC}
    in_maps = []
    for c in range(8):
        b = c % 4
        m = dict(wts)
        m.update(consts)
        m["x0"] = np.ascontiguousarray(x_sample[b])
        m["x1"] = np.ascontiguousarray(x_prompt[b])
        in_maps.append(m)
    res = run_bass_kernel_spmd(nc, in_maps, core_ids=list(range(8)))
    y_s = np.stack([res.results[b]["y0"] for b in range(4)], 0).astype(np.float32)
    y_p = np.stack([res.results[b]["y1"] for b in range(4)], 0).astype(np.float32)
    return (y_p, y_s)
```

```python
import math
import numpy as np
from contextlib import ExitStack
import concourse.bass as bass
import concourse.mybir as mybir
from concourse.bass_utils import run_bass_kernel_spmd

F32 = mybir.dt.float32
BF16 = mybir.dt.bfloat16
AF = mybir.ActivationFunctionType
ALU = mybir.AluOpType
AX = mybir.AxisListType

D = 2048
DIN = 10240
DFF = 5632
NM = 16
EPS = 1e-6
SCALE = 128.0 ** -0.5
NEG = -30000.0
ENGS = ("pe", "act", "dve", "pool", "sp")


class Res:
    __slots__ = ("name", "w", "r", "fill", "drain", "excl")

    def __init__(self, name, excl=False):
        self.name = name
        self.excl = excl
        self.w = None
        self.r = []
        self.fill = None
        self.drain = None


class SemSlot:
    __slots__ = ("sem", "cnt", "idx")

    def __init__(self, sem, idx):
        self.sem = sem
        self.cnt = 0
        self.idx = idx


class K:
    def __init__(self, nc, es, n_dma_sems=90):
        self.nc = nc
        self.recs = {e: [] for e in ENGS}
        self.seen = {e: {} for e in ENGS}
        self.esem = {}
        for e in ("pe", "act", "dve", "pool"):
            self.esem[e] = es.enter_context(nc.semaphore("es_" + e))
        self.slots = [SemSlot(es.enter_context(nc.semaphore("ds%d" % i)), i) for i in range(n_dma_sems)]
        self.free_slots = {"hw": list(self.slots[:n_dma_sems // 2]), "sw": list(self.slots[n_dma_sems // 2:])}
        self.kind = {}
        self.nins = 0

    def get_slot(self, q):
        kind = "sw" if q == "pool" else "hw"
        s = self.free_slots[kind].pop()
        self.kind[s.idx] = kind
        return s

    def release(self, res_list):
        for r in res_list:
            for s in (r.fill, r.drain):
                if s is not None:
                    self.free_slots[self.kind[s.idx]].append(s)
            r.fill = None
            r.drain = None

    def _collect(self, eng, reads, writes):
        deps = []
        for r in reads:
            if r.w is not None:
                deps.append(r.w)
        for w in writes:
            if w.w is not None:
                deps.append(w.w)
            deps.extend(w.r)
        waits = []
        seen = self.seen[eng]
        for d in deps:
            if d[0] == "E":
                x, val = d[1], d[2]
                if x == eng and eng in ("pe", "sp"):
                    continue
                key = ("E", x)
            else:
                val = d[2]
                key = ("S", d[1].idx)
            if seen.get(key, -1) >= val:
                continue
            seen[key] = val
            waits.append(d)
        return waits

    def ins(self, eng, fn, reads=(), writes=()):
        if any(r.excl for r in reads):
            writes = tuple(writes) + tuple(r for r in reads if r.excl)
            reads = tuple(r for r in reads if not r.excl)
        waits = self._collect(eng, reads, writes)
        idx = len(self.recs[eng])
        self.recs[eng].append({"fn": fn, "waits": waits, "signal": False, "dma": None})
        ev = ("E", eng, idx)
        for r in reads:
            r.r.append(ev)
        for w in writes:
            w.w = ev
            w.r = []
        self.nins += 1
        return ev

    def dma(self, q, fn, reads=(), writes=()):
        waits = self._collect(q, reads, writes)
        if writes:
            tgt = writes[0]
            if tgt.fill is None:
                tgt.fill = self.get_slot(q)
            slot = tgt.fill
        else:
            tgt = reads[0]
            if tgt.drain is None:
                tgt.drain = self.get_slot(q)
            slot = tgt.drain
        slot.cnt += 16
        ev = ("S", slot, slot.cnt)
        self.recs[q].append({"fn": fn, "waits": waits, "signal": False, "dma": slot})
        for r in reads:
            r.r.append(ev)
        for w in writes:
            w.w = ev
            w.r = []
        self.nins += 1
        return ev

    def barrier(self):
        last = {}
        for e in ("pe", "act", "dve", "pool"):
            for j in range(len(self.recs[e]) - 1, -1, -1):
                rec = self.recs[e][j]
                if rec["dma"] is None and rec["fn"] is not None:
                    last[e] = j
                    break
        for e in ENGS:
            waits = []
            seen = self.seen[e]
            for x, j in last.items():
                if x == e:
                    continue
                key = ("E", x)
                if seen.get(key, -1) >= j:
                    continue
                seen[key] = j
                waits.append(("E", x, j))
            for s in self.slots:
                if s.cnt > 0:
                    key = ("S", s.idx)
                    if seen.get(key, -1) >= s.cnt:
                        continue
                    seen[key] = s.cnt
                    waits.append(("S", s, s.cnt))
            if waits:
                self.recs[e].append({"fn": None, "waits": waits, "signal": False, "dma": None})

    def emit(self):
        nc = self.nc
        recs = self.recs
        for e in ENGS:
            for rec in recs[e]:
                for d in rec["waits"]:
                    if d[0] == "E":
                        recs[d[1]][d[2]]["signal"] = True
        val = {}
        for e in ("pe", "act", "dve", "pool"):
            c = 0
            for j, rec in enumerate(recs[e]):
                if rec["signal"]:
                    c += 1
                    val[(e, j)] = c
        esem = self.esem

        def run(e, eng):
            for rec in recs[e]:
                for d in rec["waits"]:
                    if d[0] == "E":
                        eng.wait_ge(esem[d[1]], val[(d[1], d[2])])
                    else:
                        eng.wait_ge(d[1].sem, d[2])
                if rec["fn"] is None:
                    continue
                ins = rec["fn"](eng)
                if rec["dma"] is not None:
                    ins.then_inc(rec["dma"].sem, 16)
                elif rec["signal"]:
                    ins.then_inc(esem[e], 1)

        with nc.Block() as block:
            @block.tensor
            def _(eng):
                run("pe", eng)

            @block.scalar
            def _(eng):
                run("act", eng)

            @block.vector
            def _(eng):
                run("dve", eng)

            @block.gpsimd
            def _(eng):
                run("pool", eng)

            @block.sync
            def _(eng):
                run("sp", eng)


WSPEC = [
    ("meta_tokens", (NM, D)), ("norm_mix_pre", (2, D)), ("w_in", (2, D, DIN)),
    ("lam_q1", (2, 128)), ("lam_k1", (2, 128)), ("lam_q2", (2, 128)), ("lam_k2", (2, 128)),
    ("subln", (2, 256)), ("rpb", (2, 8, 15, 31)), ("w_br_a", (2, 1024, D)), ("w_br_b", (2, 1024, D)),
    ("w_out", (2, D, D)), ("norm_mix_post", (2, D)), ("norm_ffn_pre", (2, D)),
    ("w_ffn_up", (2, D, 2 * DFF)), ("conv_w", (2, 3, DFF)), ("conv_b", (2, DFF)),
    ("w_ffn_down", (2, DFF, D)), ("norm_ffn_post", (2, D)),
]


def lam_init(l):
    return 0.8 - 0.6 * math.exp(-0.3 * l)


def build_program(jobs, depth=2, phases=None, dbg=()):
    nc = bass.Bass("TRN2", target_bir_lowering=False)
    Wd = {n: nc.dram_tensor(n, list(s), F32, kind="ExternalInput") for n, s in WSPEC}
    W = {n: t.ap() for n, t in Wd.items()}
    xin = [nc.dram_tensor("x%d" % j, [T, D], F32, kind="ExternalInput").ap() for j, T in enumerate(jobs)]
    yout = [nc.dram_tensor("y%d" % j, [T, D], F32, kind="ExternalOutput").ap() for j, T in enumerate(jobs)]
    cs_in = [nc.dram_tensor("cs%d" % j, [2, 128, T + NM], F32, kind="ExternalInput").ap() for j, T in enumerate(jobs)]
    c_rot = nc.dram_tensor("c_rot", [128, 128], F32, kind="ExternalInput").ap()
    c_j64 = nc.dram_tensor("c_j64", [64, 64], F32, kind="ExternalInput").ap()
    c_mask = nc.dram_tensor("c_mask", [64, 512], F32, kind="ExternalInput").ap()

    wbf = {}
    for l in range(depth):
        wbf["in", l] = nc.dram_tensor("wb_in%d" % l, [20, 128, 16, 512], BF16).ap()
        wbf["bra", l] = nc.dram_tensor("wb_bra%d" % l, [4, 128, 8, 512], BF16).ap()
        wbf["brb", l] = nc.dram_tensor("wb_brb%d" % l, [4, 128, 8, 512], BF16).ap()
        wbf["out", l] = nc.dram_tensor("wb_out%d" % l, [4, 128, 16, 512], BF16).ap()
        wbf["up", l] = nc.dram_tensor("wb_up%d" % l, [22, 128, 16, 512], BF16).ap()
        wbf["dn", l] = nc.dram_tensor("wb_dn%d" % l, [16, 128, 11, 512], BF16).ap()
    rpbpad_t = nc.dram_tensor("rpbpad", [depth * 8 * 15 * 31 + 256], F32)

    SC = []
    for j, T in enumerate(jobs):
        L = T + NM
        s = {}
        def dk(n):
            return "ExternalOutput" if n in dbg else "Internal"
        s["xA"] = nc.dram_tensor("xA%d" % j, [16, 128, L], F32, kind=dk("xA")).ap()
        s["xB"] = nc.dram_tensor("xB%d" % j, [16, 128, L], F32, kind=dk("xB")).ap()
        for n in ("qaT", "kaT", "qbT", "kbT", "oaT", "obT"):
            s[n] = nc.dram_tensor("%s%d" % (n, j), [8, 128, L], BF16, kind=dk(n)).ap()
        for n in ("gaT", "gbT"):
            s[n] = nc.dram_tensor("%s%d" % (n, j), [16, 128, L], BF16, kind=dk(n)).ap()
        for n in ("va", "vb"):
            s[n] = nc.dram_tensor("%s%d" % (n, j), [L, 1024], BF16, kind=dk(n)).ap()
        SC.append(s)

    es = ExitStack()
    with es:
        k = K(nc, es)

        uniq = [0]

        def sbuf(stack, name, shape, dt):
            uniq[0] += 1
            return stack.enter_context(nc.sbuf_tensor("%s_%d" % (name, uniq[0]), list(shape), dt))

        def I(eng, method, reads, writes, *a, **kw):
            return k.ins(eng, lambda e: getattr(e, method)(*a, **kw), reads, writes)

        def DMA(q, out, in_, reads=(), writes=(), **kw):
            return k.dma(q, lambda e: e.dma_start(out=out, in_=in_, **kw), reads, writes)

        def MM(out, lhsT, rhs, start, stop, reads, writes):
            return k.ins("pe", lambda e: e.matmul(out, lhsT=lhsT, rhs=rhs, start=start, stop=stop), reads, writes)

        bank = [es.enter_context(nc.psum_tensor("bank%d" % i, [128, 512], F32)) for i in range(8)]
        rb = [Res("bank%d" % i, excl=True) for i in range(8)]
        ident = sbuf(es, "ident", [128, 128], F32)
        identb = sbuf(es, "identb", [128, 128], BF16)
        ones = sbuf(es, "ones", [128, 128], F32)
        rot32 = sbuf(es, "rot32", [128, 128], F32)
        rotb = sbuf(es, "rotb", [128, 128], BF16)
        j64_32 = sbuf(es, "j64_32", [64, 64], F32)
        j64b = sbuf(es, "j64b", [64, 64], BF16)
        mask32 = sbuf(es, "mask32", [64, 512], F32)
        maskb = sbuf(es, "maskb", [64, 512], BF16)
        gam = sbuf(es, "gam", [128, depth, 4, 16], F32)
        cw = sbuf(es, "cw", [128, depth, 4, 44], F32)
        subl = sbuf(es, "subl", [128, depth, 256], F32)
        lamv = sbuf(es, "lamv", [128, depth, 4, 128], F32)
        lams = sbuf(es, "lams", [128, depth, 4], F32)
        neglam = sbuf(es, "neglam", [128, depth], F32)
        r_c = Res("consts")
        r_c2 = Res("consts2")
        cres = [r_c, r_c2]

        I("pool", "memset", (), (r_c,), ident[:], 0.0)
        I("pool", "affine_select", (r_c,), (r_c,), out=ident[:], in_=ident[:], pattern=[[-1, 128]],
          compare_op=ALU.not_equal, fill=1.0, base=0, channel_multiplier=1)
        I("pool", "memset", (), (r_c2,), ones[:], 1.0)
        DMA("sp", rot32[:], c_rot, (), (r_c2,))
        DMA("sp", j64_32[:], c_j64, (), (r_c2,))
        DMA("sp", mask32[:], c_mask, (), (r_c2,))
        norm_names = ["norm_mix_pre", "norm_mix_post", "norm_ffn_pre", "norm_ffn_post"]
        for l in range(depth):
            for i, nn in enumerate(norm_names):
                DMA("sp", gam[:, l, i, :], W[nn][l].rearrange("(c p) -> p c", p=128), (), (r_c2,), allow_slow_non_contiguous=True)
            for t in range(3):
                DMA("sp", cw[:, l, t, :], W["conv_w"][l, t].rearrange("(c p) -> p c", p=128), (), (r_c2,), allow_slow_non_contiguous=True)
            DMA("sp", cw[:, l, 3, :], W["conv_b"][l].rearrange("(c p) -> p c", p=128), (), (r_c2,), allow_slow_non_contiguous=True)
        for l in range(depth):
            DMA("sp", subl[:, l, :], bass.AP(Wd["subln"], l * 256, [[0, 128], [1, 256]]), (), (r_c2,))
            for i, nn in enumerate(["lam_q1", "lam_k1", "lam_q2", "lam_k2"]):
                DMA("sp", lamv[:, l, i, :], bass.AP(Wd[nn], l * 128, [[0, 128], [1, 128]]), (), (r_c2,))
        I("dve", "tensor_copy", (r_c,), (r_c,), out=identb[:], in_=ident[:])
        I("dve", "tensor_copy", (r_c2,), (r_c2,), out=rotb[:], in_=rot32[:])
        I("dve", "tensor_copy", (r_c2,), (r_c2,), out=j64b[:], in_=j64_32[:])
        I("dve", "tensor_copy", (r_c2,), (r_c2,), out=maskb[:], in_=mask32[:])
        for l in range(depth):
            lam0 = lam_init(l)
            I("dve", "tensor_scalar", (r_c2,), (r_c2,), out=subl[:, l, :], in0=subl[:, l, :], scalar1=float(1.0 - lam0),
              scalar2=None, op0=ALU.mult)
            I("dve", "tensor_tensor", (r_c2,), (r_c2,), out=lamv[:, l, 0, :], in0=lamv[:, l, 0, :], in1=lamv[:, l, 1, :], op=ALU.mult)
            I("dve", "tensor_tensor", (r_c2,), (r_c2,), out=lamv[:, l, 2, :], in0=lamv[:, l, 2, :], in1=lamv[:, l, 3, :], op=ALU.mult)
            I("dve", "tensor_reduce", (r_c2,), (r_c2,), out=lams[:, l, 0:1], in_=lamv[:, l, 0, :], axis=AX.X, op=ALU.add)
            I("dve", "tensor_reduce", (r_c2,), (r_c2,), out=lams[:, l, 1:2], in_=lamv[:, l, 2, :], axis=AX.X, op=ALU.add)
            I("act", "activation", (r_c2,), (r_c2,), out=lams[:, l, 0:2], in_=lams[:, l, 0:2], func=AF.Exp)
            I("dve", "scalar_tensor_tensor", (r_c2,), (r_c2,), out=neglam[:, l:l + 1], in0=lams[:, l, 1:2], scalar=float(-lam0),
              in1=lams[:, l, 0:1], op0=ALU.add, op1=ALU.subtract)

        rpbpad = rpbpad_t.ap()
        zt = sbuf(es, "zt", [1, 128], F32)
        I("pool", "memset", (), (r_c2,), zt[:], 0.0)
        nrpb = depth * 8 * 15 * 31
        DMA("pool", bass.AP(rpbpad_t, 0, [[0, 1], [1, 128]]), zt[:], (r_c2,), ())
        DMA("pool", bass.AP(rpbpad_t, 128 + nrpb, [[0, 1], [1, 128]]), zt[:], (r_c2,), ())
        r_dummy = Res("dummy")
        cres.append(r_dummy)
        DMA("pool", bass.AP(rpbpad_t, 128, [[0, 1], [1, nrpb]]),
            bass.AP(Wd["rpb"], 0, [[0, 1], [1, nrpb]]), (), (r_dummy,))

        for l in range(depth):
            for g in range(20):
                DMA("pool", wbf["in", l][g], W["w_in"][l][:, g * 512:(g + 1) * 512].rearrange("(kc p) n -> p kc n", p=128), (), (r_dummy,))
            for g in range(4):
                DMA("pool", wbf["bra", l][g], W["w_br_a"][l][:, g * 512:(g + 1) * 512].rearrange("(kc p) n -> p kc n", p=128), (), (r_dummy,))
                DMA("pool", wbf["brb", l][g], W["w_br_b"][l][:, g * 512:(g + 1) * 512].rearrange("(kc p) n -> p kc n", p=128), (), (r_dummy,))
                DMA("pool", wbf["out", l][g], W["w_out"][l][:, g * 512:(g + 1) * 512].rearrange("(kc p) n -> p kc n", p=128), (), (r_dummy,))
            for g in range(22):
                DMA("pool", wbf["up", l][g], W["w_ffn_up"][l][:, g * 512:(g + 1) * 512].rearrange("(kc p) n -> p kc n", p=128), (), (r_dummy,))
            for go in range(4):
                for kr in range(4):
                    DMA("pool", wbf["dn", l][go * 4 + kr],
                        W["w_ffn_down"][l][kr * 1408:(kr + 1) * 1408, go * 512:(go + 1) * 512].rearrange("(kc p) n -> p kc n", p=128),
                        (), (r_dummy,))
        k.barrier()

        class Ring:
            def __init__(self, stack, name, n, shape, dt, plist):
                self.t = [sbuf(stack, "%s%d" % (name, i), shape, dt) for i in range(n)]
                self.r = [Res("%s%d" % (name, i)) for i in range(n)]
                plist.extend(self.r)
                self.i = 0
                self.n = n

            def next(self):
                i = self.i % self.n
                self.i += 1
                return self.t[i], self.r[i]

        evac_flip = [0]

        def rms_stats(src, r_src, nch, N, sq_ring, rstd, r_rstd, bnk):
            for c in range(nch):
                sq, r_sq = sq_ring.next()
                I("act", "activation", (r_src,), (r_sq,), out=sq[:, :N], in_=src[:, c, :N], func=AF.Square)
                MM(bank[bnk][:, :N], ones[:], sq[:, :N], c == 0, c == nch - 1, (r_sq, r_c2), (rb[bnk],))
            I("act", "activation", (rb[bnk],), (r_rstd,), out=rstd[:, :N], in_=bank[bnk][:, :N], func=AF.Sqrt,
              scale=1.0 / (nch * 128), bias=EPS)
            I("dve", "reciprocal", (r_rstd,), (r_rstd,), out=rstd[:, :N], in_=rstd[:, :N])

        def phase_in(j):
            T = jobs[j]
            pl = []
            with ExitStack() as st:
                xr = Ring(st, "p0x", 2, [128, D], F32, pl)
                xo = Ring(st, "p0o", 2, [128, 16, 128], F32, pl)
                blocks = [(W["meta_tokens"], 0, NM, 0)] + [(xin[j], b * 128, 128, NM + b * 128) for b in range(T // 128)]
                for src, r0, n, pos in blocks:
                    xt, r_xt = xr.next()
                    DMA("sp", xt[:n, :], src[r0:r0 + n, :], (), (r_xt,))
                    ot, r_ot = xo.next()
                    for q4 in range(4):
                        b = q4 % 2
                        for i in range(4):
                            c = q4 * 4 + i
                            k.ins("pe", lambda e, b=b, i=i, c=c, xt=xt, n=n: e.transpose(
                                out=bank[b][:, i * 128:i * 128 + n], in_=xt[:n, c * 128:(c + 1) * 128], identity=ident[:n, :n]),
                                (r_xt, r_c), (rb[b],))
                        src_ap = bank[b][:].rearrange("p (i t) -> p i t", i=4)[:, :, :n]
                        if q4 % 2 == 0:
                            I("act", "activation", (rb[b],), (r_ot,), out=ot[:, q4 * 4:(q4 + 1) * 4, :n], in_=src_ap, func=AF.Copy)
                        else:
                            I("dve", "tensor_copy", (rb[b],), (r_ot,), out=ot[:, q4 * 4:(q4 + 1) * 4, :n], in_=src_ap)
                    DMA("pool", SC[j]["xA"][:, :, pos:pos + n].rearrange("c p t -> p c t"), ot[:, :, :n], (r_ot,), ())
                k.barrier()
            k.release(pl)

        def phase_out(j):
            T = jobs[j]
            pl = []
            with ExitStack() as st:
                xr = Ring(st, "p6x", 2, [128, 16, 128], F32, pl)
                xo = Ring(st, "p6o", 2, [128, D], F32, pl)
                for b in range(T // 128):
                    pos = NM + b * 128
                    xt, r_xt = xr.next()
                    DMA("sp", xt[:], SC[j]["xA"][:, :, pos:pos + 128].rearrange("c p t -> p c t"), (), (r_xt,))
                    ot, r_ot = xo.next()
                    for q4 in range(4):
                        bb = q4 % 2
                        for i in range(4):
                            c = q4 * 4 + i
                            k.ins("pe", lambda e, bb=bb, i=i, c=c, xt=xt: e.transpose(
                                out=bank[bb][:, i * 128:(i + 1) * 128], in_=xt[:, c, :], identity=ident[:]),
                                (r_xt, r_c), (rb[bb],))
                        if q4 % 2 == 0:
                            I("act", "activation", (rb[bb],), (r_ot,), out=ot[:, q4 * 512:(q4 + 1) * 512], in_=bank[bb][:], func=AF.Copy)
                        else:
                            I("dve", "tensor_copy", (rb[bb],), (r_ot,), out=ot[:, q4 * 512:(q4 + 1) * 512], in_=bank[bb][:])
                    DMA("pool", yout[j][b * 128:(b + 1) * 128, :], ot[:], (r_ot,), ())
                k.barrier()
            k.release(pl)

        def pos_tiles(T):
            return [(0, NM)] + [(NM + i * 512, 512) for i in range(T // 512)]

        def phase_proj(j, l):
            T = jobs[j]
            L = T + NM
            S = SC[j]
            pl = []
            with ExitStack() as st:
                xt = sbuf(st, "p1x", [128, 16, 512], F32); r_xt = Res("p1x")
                hT = sbuf(st, "p1h", [128, 16, 512], BF16); r_hT = Res("p1h")
                rstd = sbuf(st, "p1r", [128, 512], F32); r_rstd = Res("p1r")
                cs = sbuf(st, "p1cs", [128, 2, 512], F32); r_cs = Res("p1cs")
                pl.extend([r_xt, r_hT, r_rstd, r_cs])
                sqr = Ring(st, "p1sq", 2, [128, 512], F32, pl)
                slab = Ring(st, "p1w", 4, [128, 16, 512], BF16, pl)
                stg = Ring(st, "p1st", 3, [128, 4, 512], BF16, pl)
                qraw = Ring(st, "p1q", 2, [128, 512], BF16, pl)
                t1r = Ring(st, "p1t1", 2, [128, 512], F32, pl)
                t2r = Ring(st, "p1t2", 2, [128, 512], F32, pl)
                pb = [0]

                def nextbank():
                    b = 1 + (pb[0] % 5)
                    pb[0] += 1
                    return b

                for (p0, N) in pos_tiles(T):
                    DMA("sp", xt[:, :, :N], S["xA"][:, :, p0:p0 + N].rearrange("c p t -> p c t"), (), (r_xt,))
                    DMA("sp", cs[:, :, :N], cs_in[j][:, :, p0:p0 + N].rearrange("a p t -> p a t"), (), (r_cs,))
                    rms_stats(xt, r_xt, 16, N, sqr, rstd, r_rstd, 0)
                    for c in range(16):
                        I("dve", "scalar_tensor_tensor", (r_xt, r_rstd, r_c2), (r_hT,), out=hT[:, c, :N], in0=xt[:, c, :N],
                          scalar=gam[:, l, 0, c:c + 1], in1=rstd[:, :N], op0=ALU.mult, op1=ALU.mult)
                    for g in range(20):
                        wt, r_wt = slab.next()
                        DMA("sp", wt[:], wbf["in", l][g], (), (r_wt,))
                        if g in (4, 5, 10, 11):
                            dst = S["va"] if g < 6 else S["vb"]
                            col0 = (g - 4) * 512 if g < 6 else (g - 10) * 512
                            for s0 in range(0, N, 128):
                                n = min(128, N - s0)
                                b = nextbank()
                                for kc in range(16):
                                    MM(bank[b][:n, :], hT[:, kc, s0:s0 + n], wt[:, kc, :], kc == 0, kc == 15, (r_hT, r_wt), (rb[b],))
                                so, r_so = qraw.next()
                                I("act", "activation", (rb[b],), (r_so,), out=so[:n, :], in_=bank[b][:n, :], func=AF.Copy)
                                DMA("pool", dst[p0 + s0:p0 + s0 + n, col0:col0 + 512], so[:n, :], (r_so,), ())
                            continue
                        so, r_so = stg.next()
                        for oc in range(4):
                            b = nextbank()
                            for kc in range(16):
                                MM(bank[b][:, :N], wt[:, kc, oc * 128:(oc + 1) * 128], hT[:, kc, :N], kc == 0, kc == 15, (r_hT, r_wt), (rb[b],))
                            if g < 4:
                                qr, r_qr = qraw.next()
                                I("act", "activation", (rb[b],), (r_qr,), out=qr[:, :N], in_=bank[b][:, :N], func=AF.Copy)
                                MM(bank[7][:, :N], rotb[:], qr[:, :N], True, True, (r_qr, r_c2), (rb[7],))
                                t1, r_t1 = t1r.next()
                                t2, r_t2 = t2r.next()
                                I("dve", "tensor_tensor", (rb[b], r_cs), (r_t1,), out=t1[:, :N], in0=bank[b][:, :N], in1=cs[:, 0, :N], op=ALU.mult)
                                I("dve", "tensor_tensor", (rb[7], r_cs), (r_t2,), out=t2[:, :N], in0=bank[7][:, :N], in1=cs[:, 1, :N], op=ALU.mult)
                                I("pool", "tensor_tensor", (r_t1, r_t2), (r_so,), out=so[:, oc, :N], in0=t1[:, :N], in1=t2[:, :N], op=ALU.add)
                            elif g in (6, 7):
                                I("act", "activation", (rb[b],), (r_so,), out=so[:, oc, :N], in_=bank[b][:, :N], func=AF.Copy, scale=SCALE)
                            elif g in (8, 9):
                                I("act", "activation", (rb[b],), (r_so,), out=so[:, oc, :N], in_=bank[b][:, :N], func=AF.Copy)
                            else:
                                I("act", "activation", (rb[b],), (r_so,), out=so[:, oc, :N], in_=bank[b][:, :N], func=AF.Sigmoid)
                        if g < 2:
                            dst = S["qaT"][g * 4:(g + 1) * 4]
                        elif g < 4:
                            dst = S["kaT"][(g - 2) * 4:(g - 1) * 4]
                        elif g < 8:
                            dst = S["qbT"][(g - 6) * 4:(g - 5) * 4]
                        elif g < 10:
                            dst = S["kbT"][(g - 8) * 4:(g - 7) * 4]
                        elif g < 16:
                            dst = S["gaT"][(g - 12) * 4:(g - 11) * 4]
                        else:
                            dst = S["gbT"][(g - 16) * 4:(g - 15) * 4]
                        DMA("pool", dst[:, :, p0:p0 + N].rearrange("c p t -> p c t"), so[:, :, :N], (r_so,), ())
                k.barrier()
            k.release(pl)

        def phase_diff(j, l):
            T = jobs[j]
            L = T + NM
            S = SC[j]
            nkb = (L + 127) // 128
            kbs = [(i * 128, min(128, L - i * 128)) for i in range(nkb)]
            pl = []
            with ExitStack() as st:
                vp = sbuf(st, "p2v", [128, nkb, 257], BF16); r_vp = Res("p2v")
                kT = sbuf(st, "p2k", [128, 2, L], BF16); r_kT = Res("p2k")
                qT = sbuf(st, "p2q", [128, 2, L], BF16); r_qT = Res("p2q")
                stt = sbuf(st, "p2s", [128, 2, 2, 40], F32); r_stt = Res("p2s")
                negc = sbuf(st, "p2c", [128, 2], F32); r_negc = Res("p2c")
                pl.extend([r_vp, r_kT, r_qT, r_stt, r_negc])
                sqr = Ring(st, "p2sq", 2, [128, 512], F32, pl)
                ptr = Ring(st, "p2p", 6, [128, 512], BF16, pl)
                accs = Ring(st, "p2a", 4, [128, 4, 257], F32, pl)
                otr = Ring(st, "p2o", 2, [128, 256], F32, pl)
                o2r = Ring(st, "p2o2", 2, [128, 256], F32, pl)
                obr = Ring(st, "p2ob", 2, [128, 4, 256], BF16, pl)
                oTr = Ring(st, "p2oT", 2, [128, 2, 512], BF16, pl)
                smr = Ring(st, "p2sm", 4, [128, 8], F32, pl)
                I("pool", "memset", (), (r_vp,), vp[:, :, 256:257], 1.0)
                sb_i = [0]
                for h in range(4):
                    for c in range(2):
                        DMA("sp", kT[:, c, :], S["kaT"][h * 2 + c], (), (r_kT,))
                        DMA("sp", qT[:, c, :], S["qaT"][h * 2 + c], (), (r_qT,))
                    for (k0, kn) in kbs:
                        DMA("sp", vp[:kn, k0 // 128, 0:256], S["va"][k0:k0 + kn, h * 256:(h + 1) * 256], (), (r_vp,))
                    chunks = [(i * 512, min(512, L - i * 512)) for i in range((L + 511) // 512)]
                    for c in range(2):
                        for wi, (src, r_src) in enumerate(((kT, r_kT), (qT, r_qT))):
                            for ci, (c0, cn) in enumerate(chunks):
                                sq, r_sq = sqr.next()
                                I("act", "activation", (r_src,), (r_sq,), out=sq[:, :cn], in_=src[:, c, c0:c0 + cn], func=AF.Square)
                                MM(bank[0][:, :cn], ones[:], sq[:, :cn], True, True, (r_sq, r_c2), (rb[0],))
                                I("dve", "tensor_reduce", (rb[0],), (r_stt,), out=stt[:, c, wi, ci:ci + 1], in_=bank[0][:, :cn], axis=AX.X, op=ALU.max)
                            I("dve", "tensor_reduce", (r_stt,), (r_stt,), out=stt[:, c, wi, 39:40], in_=stt[:, c, wi, 0:len(chunks)], axis=AX.X, op=ALU.max)
                        I("dve", "tensor_tensor", (r_stt,), (r_stt,), out=stt[:, c, 0, 38:39], in0=stt[:, c, 0, 39:40], in1=stt[:, c, 1, 39:40], op=ALU.mult)
                        I("act", "activation", (r_stt,), (r_stt,), out=stt[:, c, 0, 38:39], in_=stt[:, c, 0, 38:39], func=AF.Sqrt)
                        I("dve", "tensor_scalar", (r_stt,), (r_negc,), out=negc[:, c:c + 1], in0=stt[:, c, 0, 38:39], scalar1=float(-SCALE), scalar2=None, op0=ALU.mult)
                    def epilogue(q0, N, acc):
                        nst = (N + 127) // 128
                        (a1, r_a1), (a2, r_a2) = acc
                        ob, r_ob = obr.next()
                        for s_ in range(nst):
                            n = min(128, N - s_ * 128)
                            sm, r_sm = smr.next()
                            I("dve", "reciprocal", (r_a1,), (r_sm,), out=sm[:n, 0:1], in_=a1[:n, s_, 256:257])
                            I("dve", "reciprocal", (r_a2,), (r_sm,), out=sm[:n, 1:2], in_=a2[:n, s_, 256:257])
                            I("dve", "tensor_tensor", (r_sm, r_c2), (r_sm,), out=sm[:n, 2:3], in0=sm[:n, 1:2], in1=neglam[:n, l:l + 1], op=ALU.mult)
                            ot, r_ot = otr.next()
                            o2, r_o2 = o2r.next()
                            I("dve", "tensor_scalar", (r_a2, r_sm), (r_ot,), out=ot[:n, :], in0=a2[:n, s_, 0:256], scalar1=sm[:n, 2:3], scalar2=None, op0=ALU.mult)
                            I("dve", "scalar_tensor_tensor", (r_a1, r_sm, r_ot), (r_o2,), out=o2[:n, :], in0=a1[:n, s_, 0:256], scalar=sm[:n, 0:1],
                              in1=ot[:n, :], op0=ALU.mult, op1=ALU.add)
                            I("act", "activation", (r_o2,), (r_ot, r_sm), out=ot[:n, :], in_=o2[:n, :], func=AF.Square, accum_out=sm[:n, 3:4])
                            I("act", "activation", (r_sm,), (r_sm,), out=sm[:n, 4:5], in_=sm[:n, 3:4], func=AF.Sqrt, scale=1.0 / 256, bias=EPS)
                            I("dve", "reciprocal", (r_sm,), (r_sm,), out=sm[:n, 5:6], in_=sm[:n, 4:5])
                            I("dve", "scalar_tensor_tensor", (r_o2, r_sm, r_c2), (r_ob,), out=ob[:n, s_, :], in0=o2[:n, :], scalar=sm[:n, 5:6],
                              in1=subl[:n, l, :], op0=ALU.mult, op1=ALU.mult)
                        oT, r_oT = oTr.next()
                        pbv = bank[7][:].bitcast(BF16)
                        for f in range(2):
                            for s_ in range(nst):
                                n = min(128, N - s_ * 128)
                                k.ins("pe", lambda e, f=f, s_=s_, n=n, ob=ob, pbv=pbv: e.transpose(
                                    out=pbv[:, f * 512 + s_ * 128:f * 512 + s_ * 128 + n], in_=ob[:n, s_, f * 128:(f + 1) * 128], identity=identb[:n, :n]),
                                    (r_ob, r_c), (rb[7],))
                        I("dve", "tensor_copy", (rb[7],), (r_oT,), out=oT[:, :, :N], in_=pbv.rearrange("p (f t) -> p f t", f=2)[:, :, :N])
                        DMA("pool", S["oaT"][h * 2:h * 2 + 2, :, q0:q0 + N].rearrange("c p t -> p c t"), oT[:, :, :N], (r_oT,), ())

                    accst = {}

                    def emit_A(step):
                        q0, N, c, k0, kn = step
                        sbk = 4 + (sb_i[0] % 4)
                        sb_i[0] += 1
                        MM(bank[sbk][:kn, :N], kT[:, c, k0:k0 + kn], qT[:, c, q0:q0 + N], True, True, (r_kT, r_qT), (rb[sbk],))
                        pt, r_pt = ptr.next()
                        I("act", "activation", (rb[sbk], r_negc), (r_pt,), out=pt[:kn, :N], in_=bank[sbk][:kn, :N], func=AF.Exp,
                          scale=SCALE, bias=negc[:kn, c:c + 1])
                        return (step, pt, r_pt)

                    def emit_B(item):
                        (q0, N, c, k0, kn), pt, r_pt = item
                        nst = (N + 127) // 128
                        for s_ in range(nst):
                            n = min(128, N - s_ * 128)
                            MM(bank[s_][:n, 0:257], pt[:kn, s_ * 128:s_ * 128 + n], vp[:kn, k0 // 128, :], k0 == 0, k0 == kbs[-1][0],
                               (r_pt, r_vp), (rb[s_],))
                        if k0 == kbs[-1][0]:
                            at, r_at = accs.next()
                            for s_ in range(nst):
                                n = min(128, N - s_ * 128)
                                if (s_ + c) % 2 == 0:
                                    I("act", "activation", (rb[s_],), (r_at,), out=at[:n, s_, :], in_=bank[s_][:n, 0:257], func=AF.Copy)
                                else:
                                    I("dve", "tensor_copy", (rb[s_],), (r_at,), out=at[:n, s_, :], in_=bank[s_][:n, 0:257])
                            accst.setdefault(q0, []).append((at, r_at))
                            if c == 1:
                                epilogue(q0, N, accst.pop(q0))

                    steps = [(q0, N, c, k0, kn) for (q0, N) in pos_tiles(T) for c in range(2) for (k0, kn) in kbs]
                    pend = []
                    for step in steps:
                        pend.append(emit_A(step))
                        if len(pend) > 3:
                            emit_B(pend.pop(0))
                    while pend:
                        emit_B(pend.pop(0))
                k.barrier()
            k.release(pl)

        def phase_na(j, l):
            T = jobs[j]
            L = T + NM
            S = SC[j]
            rows = T // 64
            pl = []
            with ExitStack() as st:
                hk = sbuf(st, "p3hk", [64, 8, 15, 64], BF16); r_hk = Res("p3hk")
                hk32 = sbuf(st, "p3hk32", [64, 8, 15, 64], F32); r_hk32 = Res("p3hk32")
                km = sbuf(st, "p3km", [128, 8, NM], BF16); r_km = Res("p3km")
                qm = sbuf(st, "p3qm", [128, 8, NM], BF16); r_qm = Res("p3qm")
                vm = sbuf(st, "p3vm", [NM, 8, 129], BF16); r_vm = Res("p3vm")
                pl.extend([r_hk, r_hk32, r_km, r_qm, r_vm])
                qtr = Ring(st, "p3q", 2, [128, 8, 512], BF16, pl)
                kwr = Ring(st, "p3k", 2, [128, 8, 1024], BF16, pl)
                vwr = Ring(st, "p3v", 2, [64, 16, 8, 129], BF16, pl)
                sqr = Ring(st, "p3sq", 2, [128, 512], F32, pl)
                sttr = Ring(st, "p3st", 2, [128, 64], F32, pl)
                ptr = Ring(st, "p3p", 3, [64, 512], BF16, pl)
                pmr = Ring(st, "p3pm", 3, [NM, 64], BF16, pl)
                obr = Ring(st, "p3ob", 3, [64, 8, 128], BF16, pl)
                oTr = Ring(st, "p3oT", 2, [128, 8, 512], BF16, pl)
                smr = Ring(st, "p3sm", 4, [64, 8], F32, pl)
                base = 128 + l * 8 * 15 * 31 - 48
                DMA("sp", hk32[:], bass.AP(rpbpad_t, base, [[1, 64], [465, 8], [31, 15], [1, 64]]), (), (r_hk32,))
                I("dve", "tensor_copy", (r_hk32,), (r_hk,), out=hk[:], in_=hk32[:])
                for v_ in vwr.t:
                    I("pool", "memset", (), (vwr.r[vwr.t.index(v_)],), v_[:, :, :, 128:129], 1.0)
                I("pool", "memset", (), (r_vm,), vm[:, :, 128:129], 1.0)
                DMA("sp", km[:], S["kbT"][:, :, 0:NM].rearrange("h p t -> p h t"), (), (r_km,))
                DMA("sp", qm[:], S["qbT"][:, :, 0:NM].rearrange("h p t -> p h t"), (), (r_qm,))
                DMA("sp", vm[:, :, 0:128], S["vb"][0:NM, :].rearrange("t (h d) -> t h d", h=8), (), (r_vm,))
                sb_i = [0]

                def shift_bound(parts, r_parts, stt, r_stt):
                    for wi in range(2):
                        cnt = 0
                        for (src, r_src, w) in parts[wi]:
                            for h in range(8):
                                for c0 in range(0, w, 512):
                                    cn = min(512, w - c0)
                                    sq, r_sq = sqr.next()
                                    I("act", "activation", (r_src,), (r_sq,), out=sq[:, :cn], in_=src[:, h, c0:c0 + cn], func=AF.Square)
                                    MM(bank[0][:, :cn], ones[:], sq[:, :cn], True, True, (r_sq, r_c2), (rb[0],))
                                    I("dve", "tensor_reduce", (rb[0],), (r_stt,), out=stt[:, wi * 28 + cnt:wi * 28 + cnt + 1], in_=bank[0][:, :cn], axis=AX.X, op=ALU.max)
                                    cnt += 1
                        I("dve", "tensor_reduce", (r_stt,), (r_stt,), out=stt[:, 56 + wi:57 + wi], in_=stt[:, wi * 28:wi * 28 + cnt], axis=AX.X, op=ALU.max)
                    I("dve", "tensor_tensor", (r_stt,), (r_stt,), out=stt[:, 58:59], in0=stt[:, 56:57], in1=stt[:, 57:58], op=ALU.mult)
                    I("act", "activation", (r_stt,), (r_stt,), out=stt[:, 59:60], in_=stt[:, 58:59], func=AF.Sqrt)
                    I("dve", "tensor_scalar", (r_stt,), (r_stt,), out=stt[:, 60:61], in0=stt[:, 59:60], scalar1=-1.0, scalar2=-1.0, op0=ALU.mult, op1=ALU.add)

                def att_A(h, qsrc, qc0, nq, win, stt, r_q, r_kw, r_vw, r_stt, ob, r_ob, kw=None, vw=None):
                    bS = 1 + (sb_i[0] % 2)
                    bM = 3 + (sb_i[0] % 2)
                    bO = 5 + (sb_i[0] % 2)
                    sb_i[0] += 1
                    nw = 0
                    pt = r_pt = None
                    w0 = 0
                    if win is not None:
                        w0, dr0 = win
                        nw = 8
                        MM(bank[bS][0:64, 0:512], identb[0:64, 0:64], maskb[:, :], True, False, (r_c, r_c2), (rb[bS],))
                        for i in range(8):
                            MM(bank[bS][0:64, i * 64:(i + 1) * 64], kw[:, h, (w0 + i) * 64:(w0 + i + 1) * 64], qsrc[:, h, qc0:qc0 + nq],
                               False, False, (r_kw, r_q), (rb[bS],))
                        for i in range(8):
                            MM(bank[bS][0:64, i * 64:(i + 1) * 64], hk[:, h, dr0 + i, :], j64b[:, :], False, i == 7, (r_hk, r_c2), (rb[bS],))
                    MM(bank[bM][0:NM, 0:nq], km[:, h, :], qsrc[:, h, qc0:qc0 + nq], True, True, (r_km, r_q), (rb[bM],))
                    pm, r_pm = pmr.next()
                    I("act", "activation", (rb[bM], r_stt), (r_pm,), out=pm[:, :nq], in_=bank[bM][0:NM, 0:nq], func=AF.Exp, bias=stt[0:NM, 60:61])
                    if nw:
                        pt, r_pt = ptr.next()
                        I("act", "activation", (rb[bS], r_stt), (r_pt,), out=pt[:, :], in_=bank[bS][0:64, 0:512], func=AF.Exp, bias=stt[0:64, 60:61])
                    return (h, nq, nw, w0, bO, pm, r_pm, pt, r_pt, vw, r_vw, ob, r_ob)

                def att_B(rec):
                    h, nq, nw, w0, bO, pm, r_pm, pt, r_pt, vw, r_vw, ob, r_ob = rec
                    if nw:
                        for i in range(8):
                            MM(bank[bO][0:nq, 0:129], pt[:, i * 64:(i + 1) * 64], vw[:, w0 + i, h, :], i == 0, False, (r_pt, r_vw), (rb[bO],))
                    MM(bank[bO][0:nq, 0:129], pm[:, :nq], vm[:, h, :], nw == 0, True, (r_pm, r_vm), (rb[bO],))
                    sm, r_sm = smr.next()
                    I("dve", "reciprocal", (rb[bO],), (r_sm,), out=sm[:nq, 0:1], in_=bank[bO][0:nq, 128:129])
                    I("dve", "tensor_scalar", (rb[bO], r_sm), (r_ob,), out=ob[:nq, h, :], in0=bank[bO][0:nq, 0:128], scalar1=sm[:nq, 0:1], scalar2=None, op0=ALU.mult)

                def attend(qsrc, qc0, nq, win, stt, r_q, r_kw, r_vw, r_stt, ob, r_ob, kw=None, vw=None):
                    for h in range(8):
                        att_B(att_A(h, qsrc, qc0, nq, win, stt, r_q, r_kw, r_vw, r_stt, ob, r_ob, kw=kw, vw=vw))

                def flush(ob, r_ob, nq, oT, r_oT, col0):
                    pbv = bank[7][:].bitcast(BF16)
                    for h in range(8):
                        k.ins("pe", lambda e, h=h, ob=ob, nq=nq, pbv=pbv: e.transpose(
                            out=pbv[:, h * 64:h * 64 + nq], in_=ob[:nq, h, :], identity=identb[:nq, :nq]), (r_ob, r_c), (rb[7],))
                    I("dve", "tensor_copy", (rb[7],), (r_oT,), out=oT[:, :, col0:col0 + nq], in_=pbv[:, 0:512].rearrange("p (h t) -> p h t", h=8)[:, :, :nq])

                stt, r_stt = sttr.next()
                shift_bound([[(qm, r_qm, NM)], [(km, r_km, NM)]], None, stt, r_stt)
                ob, r_ob = obr.next()
                oT, r_oT = oTr.next()
                attend(qm, 0, NM, None, stt, r_qm, None, None, r_stt, ob, r_ob)
                flush(ob, r_ob, NM, oT, r_oT, 0)
                DMA("pool", S["obT"][:, :, 0:NM].rearrange("h p t -> p h t"), oT[:, :, 0:NM], (r_oT,), ())
                for ti in range(T // 512):
                    r0 = ti * 8
                    wr0 = max(0, min(r0 - 4, rows - 16))
                    wr0 = min(wr0, max(0, rows - 16))
                    nwr = min(16, rows - wr0)
                    qt, r_qt = qtr.next()
                    kw, r_kw = kwr.next()
                    vw, r_vw = vwr.next()
                    p0 = NM + ti * 512
                    DMA("sp", qt[:], S["qbT"][:, :, p0:p0 + 512].rearrange("h p t -> p h t"), (), (r_qt,))
                    DMA("sp", kw[:, :, 0:nwr * 64], S["kbT"][:, :, NM + wr0 * 64:NM + (wr0 + nwr) * 64].rearrange("h p t -> p h t"), (), (r_kw,))
                    for rr in range(nwr):
                        DMA("sp", vw[:, rr, :, 0:128], S["vb"][NM + (wr0 + rr) * 64:NM + (wr0 + rr + 1) * 64, :].rearrange("t (h d) -> t h d", h=8), (), (r_vw,))
                    stt, r_stt = sttr.next()
                    shift_bound([[(qt, r_qt, 512)], [(kw, r_kw, nwr * 64), (km, r_km, NM)]], None, stt, r_stt)
                    oT, r_oT = oTr.next()
                    pend = []

                    def do_B(item):
                        rec, jr_ = item
                        att_B(rec)
                        if rec[0] == 7:
                            flush(rec[11], rec[12], 64, oT, r_oT, jr_ * 64)

                    for jr in range(8):
                        r = r0 + jr
                        rs = max(0, min(r - 4, rows - 8))
                        dr0 = rs - r + 7
                        ob, r_ob = obr.next()
                        for h in range(8):
                            pend.append((att_A(h, qt, jr * 64, 64, (rs - wr0, dr0), stt, r_qt, r_kw, r_vw, r_stt, ob, r_ob, kw=kw, vw=vw), jr))
                            if len(pend) > 1:
                                do_B(pend.pop(0))
                    while pend:
                        do_B(pend.pop(0))
                    DMA("pool", S["obT"][:, :, p0:p0 + 512].rearrange("h p t -> p h t"), oT[:], (r_oT,), ())
                k.barrier()
            k.release(pl)

        def post_norm_residual(yT, r_yT, xt, r_xt, N, l, gi, rstd, r_rstd, tmpr, c_lo=0, x_off=0):
            for c in range(16):
                tm, r_tm = tmpr.next()
                I("dve", "scalar_tensor_tensor", (r_yT, r_rstd, r_c2), (r_tm,), out=tm[:, :N], in0=yT[:, c, :N], scalar=gam[:, l, gi, c:c + 1],
                  in1=rstd[:, :N], op0=ALU.mult, op1=ALU.mult)
                I("pool", "tensor_tensor", (r_tm, r_xt), (r_yT,), out=yT[:, c, :N], in0=tm[:, :N], in1=xt[:, c, x_off:x_off + N], op=ALU.add)

        def phase_mix(j, l):
            T = jobs[j]
            L = T + NM
            S = SC[j]
            pl = []
            with ExitStack() as st:
                xt = sbuf(st, "p4x", [128, 16, 512], F32); r_xt = Res("p4x")
                oa = sbuf(st, "p4oa", [128, 8, 512], BF16); r_oa = Res("p4oa")
                obt = sbuf(st, "p4ob", [128, 8, 512], BF16); r_obt = Res("p4ob")
                mix = sbuf(st, "p4m", [128, 16, 512], BF16); r_mix = Res("p4m")
                yT = sbuf(st, "p4y", [128, 16, 512], F32); r_yT = Res("p4y")
                rstd = sbuf(st, "p4r", [128, 512], F32); r_rstd = Res("p4r")
                pl.extend([r_xt, r_oa, r_obt, r_mix, r_yT, r_rstd])
                gar = Ring(st, "p4ga", 2, [128, 4, 512], BF16, pl)
                gbr = Ring(st, "p4gb", 2, [128, 4, 512], BF16, pl)
                slab = Ring(st, "p4w", 3, [128, 16, 512], BF16, pl)
                sqr = Ring(st, "p4sq", 2, [128, 512], F32, pl)
                t1r = Ring(st, "p4t1", 2, [128, 512], F32, pl)
                t2r = Ring(st, "p4t2", 2, [128, 512], F32, pl)
                pb = [0]
                for (p0, N) in pos_tiles(T):
                    DMA("sp", oa[:, :, :N], S["oaT"][:, :, p0:p0 + N].rearrange("c p t -> p c t"), (), (r_oa,))
                    DMA("sp", obt[:, :, :N], S["obT"][:, :, p0:p0 + N].rearrange("c p t -> p c t"), (), (r_obt,))
                    DMA("sp", xt[:, :, :N], S["xA"][:, :, p0:p0 + N].rearrange("c p t -> p c t"), (), (r_xt,))
                    for go in range(4):
                        wa, r_wa = slab.next()
                        DMA("sp", wa[:, 0:8, :], wbf["bra", l][go], (), (r_wa,))
                        DMA("sp", wa[:, 8:16, :], wbf["brb", l][go], (), (r_wa,))
                        ga, r_ga = gar.next()
                        gb, r_gb = gbr.next()
                        DMA("sp", ga[:, :, :N], S["gaT"][go * 4:(go + 1) * 4, :, p0:p0 + N].rearrange("c p t -> p c t"), (), (r_ga,))
                        DMA("sp", gb[:, :, :N], S["gbT"][go * 4:(go + 1) * 4, :, p0:p0 + N].rearrange("c p t -> p c t"), (), (r_gb,))
                        for oc in range(4):
                            ba = 1 + (pb[0] % 2)
                            bb = 3 + (pb[0] % 2)
                            pb[0] += 1
                            for kc in range(8):
                                MM(bank[ba][:, :N], wa[:, kc, oc * 128:(oc + 1) * 128], oa[:, kc, :N], kc == 0, kc == 7, (r_wa, r_oa), (rb[ba],))
                            for kc in range(8):
                                MM(bank[bb][:, :N], wa[:, 8 + kc, oc * 128:(oc + 1) * 128], obt[:, kc, :N], kc == 0, kc == 7, (r_wa, r_obt), (rb[bb],))
                            t1, r_t1 = t1r.next()
                            t2, r_t2 = t2r.next()
                            I("dve", "tensor_tensor", (rb[ba], r_ga), (r_t1,), out=t1[:, :N], in0=bank[ba][:, :N], in1=ga[:, oc, :N], op=ALU.mult)
                            I("dve", "tensor_tensor", (rb[bb], r_gb), (r_t2,), out=t2[:, :N], in0=bank[bb][:, :N], in1=gb[:, oc, :N], op=ALU.mult)
                            I("pool", "tensor_tensor", (r_t1, r_t2), (r_mix,), out=mix[:, go * 4 + oc, :N], in0=t1[:, :N], in1=t2[:, :N], op=ALU.add)
                    for go in range(4):
                        wo, r_wo = slab.next()
                        DMA("sp", wo[:], wbf["out", l][go], (), (r_wo,))
                        for oc in range(4):
                            c = go * 4 + oc
                            b = 5 + (pb[0] % 2)
                            pb[0] += 1
                            for kc in range(16):
                                MM(bank[b][:, :N], wo[:, kc, oc * 128:(oc + 1) * 128], mix[:, kc, :N], kc == 0, kc == 15, (r_wo, r_mix), (rb[b],))
                            if c > 0:
                                psq, r_psq, pc = pend_sq
                                MM(bank[0][:, :N], ones[:], psq[:, :N], pc == 0, False, (r_psq, r_c2), (rb[0],))
                            I("dve", "tensor_copy", (rb[b],), (r_yT,), out=yT[:, c, :N], in_=bank[b][:, :N])
                            sq, r_sq = sqr.next()
                            I("act", "activation", (rb[b],), (r_sq,), out=sq[:, :N], in_=bank[b][:, :N], func=AF.Square)
                            pend_sq = (sq, r_sq, c)
                            if c == 15:
                                MM(bank[0][:, :N], ones[:], sq[:, :N], False, True, (r_sq, r_c2), (rb[0],))
                    I("act", "activation", (rb[0],), (r_rstd,), out=rstd[:, :N], in_=bank[0][:, :N], func=AF.Sqrt, scale=1.0 / D, bias=EPS)
                    I("dve", "reciprocal", (r_rstd,), (r_rstd,), out=rstd[:, :N], in_=rstd[:, :N])
                    post_norm_residual(yT, r_yT, xt, r_xt, N, l, 1, rstd, r_rstd, t1r)
                    DMA("pool", S["xB"][:, :, p0:p0 + N].rearrange("c p t -> p c t"), yT[:, :, :N], (r_yT,), ())
                k.barrier()
            k.release(pl)

        def phase_ffn(j, l):
            T = jobs[j]
            L = T + NM
            S = SC[j]
            pl = []
            with ExitStack() as st:
                xt = sbuf(st, "p5x", [128, 16, 512], F32); r_xt = Res("p5x")
                hT = sbuf(st, "p5h", [128, 16, 512], BF16); r_hT = Res("p5h")
                act = sbuf(st, "p5a", [128, 44, 512], BF16); r_act = Res("p5a")
                fT = sbuf(st, "p5f", [128, 16, 512], F32); r_fT = Res("p5f")
                rstd = sbuf(st, "p5r", [128, 512], F32); r_rstd = Res("p5r")
                pl.extend([r_xt, r_hT, r_act, r_fT, r_rstd])
                slab = Ring(st, "p5w", 3, [128, 16, 512], BF16, pl)
                sqr = Ring(st, "p5sq", 2, [128, 512], F32, pl)
                a1r = Ring(st, "p5a1", 2, [128, 512], F32, pl)
                a2r = Ring(st, "p5a2", 2, [128, 512], F32, pl)
                pb = [0]
                s = 0
                while s < L:
                    nout = min(510, L - s)
                    Nc = nout + 2
                    lo = s - 1
                    c_lo = 1 if lo < 0 else 0
                    c_hi = Nc - 1 if lo + Nc > L else Nc
                    if c_lo > 0 or c_hi < Nc:
                        I("pool", "memset", (), (r_xt,), xt[:, :, :Nc], 0.0)
                    DMA("sp", xt[:, :, c_lo:c_hi], S["xB"][:, :, lo + c_lo:lo + c_hi].rearrange("c p t -> p c t"), (), (r_xt,))
                    rms_stats(xt, r_xt, 16, Nc, sqr, rstd, r_rstd, 0)
                    for c in range(16):
                        I("dve", "scalar_tensor_tensor", (r_xt, r_rstd, r_c2), (r_hT,), out=hT[:, c, :Nc], in0=xt[:, c, :Nc],
                          scalar=gam[:, l, 2, c:c + 1], in1=rstd[:, :Nc], op0=ALU.mult, op1=ALU.mult)
                    for g in range(11):
                        wg, r_wg = slab.next()
                        DMA("sp", wg[:], wbf["up", l][g], (), (r_wg,))
                        wv, r_wv = slab.next()
                        DMA("sp", wv[:], wbf["up", l][11 + g], (), (r_wv,))
                        for oc in range(4):
                            cc = g * 4 + oc
                            bg = 1 + (pb[0] % 2)
                            bv = 3 + (pb[0] % 2)
                            pb[0] += 1
                            for kc in range(16):
                                MM(bank[bg][:, :Nc], wg[:, kc, oc * 128:(oc + 1) * 128], hT[:, kc, :Nc], kc == 0, kc == 15, (r_wg, r_hT), (rb[bg],))
                            for kc in range(16):
                                MM(bank[bv][:, :Nc], wv[:, kc, oc * 128:(oc + 1) * 128], hT[:, kc, :Nc], kc == 0, kc == 15, (r_wv, r_hT), (rb[bv],))
                            a1, r_a1 = a1r.next()
                            a2, r_a2 = a2r.next()
                            I("dve", "tensor_scalar", (rb[bg], r_c2), (r_a1,), out=a1[:, :nout], in0=bank[bg][:, 1:1 + nout], scalar1=cw[:, l, 1, cc:cc + 1],
                              scalar2=cw[:, l, 3, cc:cc + 1], op0=ALU.mult, op1=ALU.add)
                            I("dve", "scalar_tensor_tensor", (rb[bg], r_a1, r_c2), (r_a2,), out=a2[:, :nout], in0=bank[bg][:, 0:nout], scalar=cw[:, l, 0, cc:cc + 1],
                              in1=a1[:, :nout], op0=ALU.mult, op1=ALU.add)
                            I("dve", "scalar_tensor_tensor", (rb[bg], r_a2, r_c2), (r_a1,), out=a1[:, :nout], in0=bank[bg][:, 2:2 + nout], scalar=cw[:, l, 2, cc:cc + 1],
                              in1=a2[:, :nout], op0=ALU.mult, op1=ALU.add)
                            I("act", "activation", (r_a1,), (r_a2,), out=a2[:, :nout], in_=a1[:, :nout], func=AF.Gelu_apprx_tanh)
                            I("dve", "tensor_tensor", (rb[bv], r_a2), (r_act,), out=act[:, cc, :nout], in0=bank[bv][:, 1:1 + nout], in1=a2[:, :nout], op=ALU.mult)
                    for go in range(4):
                        for kr in range(4):
                            wd, r_wd = slab.next()
                            DMA("sp", wd[:, 0:11, :], wbf["dn", l][go * 4 + kr], (), (r_wd,))
                            for oc in range(4):
                                for kc in range(11):
                                    MM(bank[4 + oc][:, :nout], wd[:, kc, oc * 128:(oc + 1) * 128], act[:, kr * 11 + kc, :nout],
                                       kr == 0 and kc == 0, kr == 3 and kc == 10, (r_wd, r_act), (rb[4 + oc],))
                        for oc in range(4):
                            c = go * 4 + oc
                            b = 4 + oc
                            I("dve", "tensor_copy", (rb[b],), (r_fT,), out=fT[:, c, :nout], in_=bank[b][:, :nout])
                            sq, r_sq = sqr.next()
                            I("act", "activation", (rb[b],), (r_sq,), out=sq[:, :nout], in_=bank[b][:, :nout], func=AF.Square)
                            MM(bank[0][:, :nout], ones[:], sq[:, :nout], c == 0, c == 15, (r_sq, r_c2), (rb[0],))
                    I("act", "activation", (rb[0],), (r_rstd,), out=rstd[:, :nout], in_=bank[0][:, :nout], func=AF.Sqrt, scale=1.0 / D, bias=EPS)
                    I("dve", "reciprocal", (r_rstd,), (r_rstd,), out=rstd[:, :nout], in_=rstd[:, :nout])
                    post_norm_residual(fT, r_fT, xt, r_xt, nout, l, 3, rstd, r_rstd, a1r, x_off=1)
                    DMA("pool", S["xA"][:, :, s:s + nout].rearrange("c p t -> p c t"), fT[:, :, :nout], (r_fT,), ())
                    s += nout
                k.barrier()
            k.release(pl)

        def on(name, l=0):
            return phases is None or name in phases or (name, l) in phases
        for j in range(len(jobs)):
            if on("in"):
                phase_in(j)
            for l in range(depth):
                if on("proj", l):
                    phase_proj(j, l)
                if on("diff", l):
                    phase_diff(j, l)
                if on("na", l):
                    phase_na(j, l)
                if on("mix", l):
                    phase_mix(j, l)
                if on("ffn", l):
                    phase_ffn(j, l)
            if on("out"):
                phase_out(j)
        k.barrier()
        k.emit()
    return nc


def host_consts(jobs):
    c = {}
    rot = np.zeros((128, 128), np.float32)
    for m in range(64):
        rot[m + 64, m] = -1.0
    for m in range(64, 128):
        rot[m - 64, m] = 1.0
    c["c_rot"] = rot
    c["c_j64"] = np.ascontiguousarray(np.eye(64, dtype=np.float32)[::-1])
    cstart = np.clip(np.arange(64) - 8, 0, 48)
    cp = np.arange(64)[:, None]
    cq = np.arange(64)[None, :]
    inwin = (cp >= cstart[None, :]) & (cp < cstart[None, :] + 16)
    m = np.where(inwin, 0.0, NEG).astype(np.float32)
    c["c_mask"] = np.ascontiguousarray(np.tile(m, (1, 8)))
    for j, T in enumerate(jobs):
        L = T + NM
        inv = (1.0 / (np.float32(10000.0) ** (np.arange(0, 128, 2, dtype=np.float32) / np.float32(128)))).astype(np.float32)
        ang = (np.arange(L, dtype=np.float32)[:, None] * inv[None, :]).astype(np.float32)
        cos = np.cos(ang).astype(np.float32).T
        sin = np.sin(ang).astype(np.float32).T
        c["cs%d" % j] = np.ascontiguousarray(np.stack([np.concatenate([cos, cos], 0), np.concatenate([sin, sin], 0)], 0))
    return c


_CACHE = {}


def kernel(**inputs):
    x_prompt = np.asarray(inputs["x_prompt"], np.float32)
    x_sample = np.asarray(inputs["x_sample"], np.float32)
    jobs = (x_sample.shape[1], x_prompt.shape[1])
    key = jobs
    if key not in _CACHE:
        _CACHE[key] = build_program(list(jobs), depth=2)
    nc = _CACHE[key]
    consts = host_consts(jobs)
    wts = {n: np.ascontiguousarray(np.asarray(inputs[n], np.float32)) for n, _ in WSPEC}
    real = [0, 1, 4, 5]
    z0 = np.zeros(x_sample.shape[1:], np.float32)
    z1 = np.zeros(x_prompt.shape[1:], np.float32)
    in_maps = []
    for c in range(8):
        m = dict(wts)
        m.update(consts)
        if c in real:
            b = real.index(c)
            m["x0"] = np.ascontiguousarray(x_sample[b])
            m["x1"] = np.ascontiguousarray(x_prompt[b])
        else:
            m["x0"] = z0
            m["x1"] = z1
        in_maps.append(m)
    res = run_bass_kernel_spmd(nc, in_maps, core_ids=list(range(8)))
    y_s = np.stack([res.results[real[b]]["y0"] for b in range(4)], 0).astype(np.float32)
    y_p = np.stack([res.results[real[b]]["y1"] for b in range(4)], 0).astype(np.float32)
    return (y_p, y_s)
```
